# Optimizing a Trainium2 kernel written in Bass

```python
import jax, jax.numpy as jnp
from jax import lax
import numpy as np

D_MODEL = 1024
BATCH = 8
SEQ = 2048
DEPTH = 1
DEC_BATCH = 16
DEC_SEQ = 2048
PAST_LEN = 128

HEAD_DIM = 64
N_HEADS_A = 8
N_HEADS_B = 8
WIDTH_A = N_HEADS_A * HEAD_DIM
WIDTH_B = N_HEADS_B * HEAD_DIM
DILATED_PATTERNS = ((128, 1), (512, 4), (2048, 16))
ROPE_THETA = 500000.0
ROPE_DIM = HEAD_DIM // 4
GRID_W = 64
NA_ROWS_MAX = 8
NA_COLS = 16
NA_COL_BLOCK = NA_COLS
NA_COL_SPAN = 2 * NA_COLS
RPB_ROWS = 2 * NA_ROWS_MAX - 1
RPB_COLS = 2 * NA_COLS - 1
D_FF = 2816
CONV_W = 3
EPS = 1e-6
NEG_INF = -1e30
IN_SPLITS = (WIDTH_A, 2 * WIDTH_A, 3 * WIDTH_A, 3 * WIDTH_A + WIDTH_B,
             3 * WIDTH_A + 2 * WIDTH_B, 3 * WIDTH_A + 3 * WIDTH_B, 3 * WIDTH_A + 3 * WIDTH_B + D_MODEL)
IN_COLS = 3 * WIDTH_A + 3 * WIDTH_B + 2 * D_MODEL

kernel_name = "hybrid_dilated_neighbourhood_encoder"


def _rms_norm(x, g):
    xf = x.astype(jnp.float32)
    y = xf * lax.rsqrt(jnp.mean(xf * xf, axis=-1, keepdims=True) + EPS)
    return (y * g.astype(jnp.float32)).astype(x.dtype)


def _heads(t, n_heads):
    return t.reshape(t.shape[0], t.shape[1], n_heads, HEAD_DIM)


def _partial_rope(x, pos):
    half = ROPE_DIM // 2
    inv = jnp.power(jnp.float32(ROPE_THETA), -jnp.arange(half, dtype=jnp.float32) / half)
    ang = pos.astype(jnp.float32)[:, None] * inv[None, :]
    cos = jnp.cos(ang)[None, :, None, :]
    sin = jnp.sin(ang)[None, :, None, :]
    xr = x[..., :ROPE_DIM].astype(jnp.float32)
    x1, x2 = xr[..., :half], xr[..., half:]
    rot = jnp.concatenate([x1 * cos - x2 * sin, x1 * sin + x2 * cos], axis=-1).astype(x.dtype)
    return jnp.concatenate([rot, x[..., ROPE_DIM:]], axis=-1)


def _banded_window_attention(q, k, v, half):
    n, L, H, dh = q.shape
    blk = half
    nb = -(-L // blk)
    pad = nb * blk - L
    qb = jnp.pad(q, ((0, 0), (0, pad), (0, 0), (0, 0))).reshape(n, nb, blk, H, dh)

    def band(t):
        tb = jnp.pad(t, ((0, 0), (blk, pad + blk), (0, 0), (0, 0))).reshape(n, nb + 2, blk, H, dh)
        return jnp.concatenate([tb[:, :-2], tb[:, 1:-1], tb[:, 2:]], axis=2)

    kw, vw = band(k), band(v)
    s = jnp.einsum("nbqhd,nbkhd->nbhqk", qb, kw, preferred_element_type=jnp.float32) * (HEAD_DIM ** -0.5)
    qi = jnp.arange(nb)[:, None] * blk + jnp.arange(blk)[None, :]
    kj = jnp.arange(nb)[:, None] * blk - blk + jnp.arange(3 * blk)[None, :]
    valid = ((jnp.abs(kj[:, None, :] - qi[:, :, None]) <= half)
             & (kj[:, None, :] >= 0) & (kj[:, None, :] < L))
    s = jnp.where(valid[None, :, None], s, NEG_INF)
    m = jnp.max(s, axis=-1, keepdims=True)
    p = jnp.exp(s - m)
    den = jnp.sum(p, axis=-1, keepdims=True)
    o = jnp.einsum("nbhqk,nbkhd->nbqhd", (p / den).astype(v.dtype), vw)
    lse = (m + jnp.log(den))[..., 0]
    o = o.reshape(n, nb * blk, H, dh)[:, :L]
    lse = lse.transpose(0, 1, 3, 2).reshape(n, nb * blk, H)[:, :L]
    return o, lse


def _dilated_mixture_attention(q, k, v):
    B, S, H, dh = q.shape
    outs, lses = [], []
    for window, dil in DILATED_PATTERNS:
        half = window // (2 * dil)
        L = S // dil

        def to_res(t):
            return t.reshape(B, L, dil, H, dh).transpose(0, 2, 1, 3, 4).reshape(B * dil, L, H, dh)

        o, lse = _banded_window_attention(to_res(q), to_res(k), to_res(v), half)
        outs.append(o.reshape(B, dil, L, H, dh).transpose(0, 2, 1, 3, 4).reshape(B, S, H, dh))
        lses.append(lse.reshape(B, dil, L, H).transpose(0, 2, 1, 3).reshape(B, S, H))
    w = jax.nn.softmax(jnp.stack(lses), axis=0)
    out = jnp.einsum("pbsh,pbshd->bshd", w, jnp.stack(outs).astype(jnp.float32))
    return out.astype(q.dtype)


def _neighbourhood_attention(q, k, v, rpb):
    B, S, H, dh = q.shape
    rows = S // GRID_W
    kh = min(NA_ROWS_MAX, rows)
    ncb = GRID_W // NA_COL_BLOCK
    col_q = jnp.arange(GRID_W).reshape(ncb, NA_COL_BLOCK)
    cs = jnp.clip(col_q - NA_COLS // 2, 0, GRID_W - NA_COLS)
    span0 = jnp.clip(jnp.arange(ncb) * NA_COL_BLOCK - NA_COLS // 2, 0, GRID_W - NA_COL_SPAN)
    col_k = span0[:, None] + jnp.arange(NA_COL_SPAN)[None, :]
    col_ok = (col_k[:, None, :] >= cs[:, :, None]) & (col_k[:, None, :] < cs[:, :, None] + NA_COLS)
    col_rel = jnp.clip(col_k[:, None, :] - col_q[:, :, None] + NA_COLS - 1, 0, RPB_COLS - 1)
    qg = q.reshape(B, rows, ncb, NA_COL_BLOCK, H, dh)
    kg = k.reshape(B, rows, GRID_W, H, dh)[:, :, col_k]
    vg = v.reshape(B, rows, GRID_W, H, dh)[:, :, col_k]
    scale = HEAD_DIM ** -0.5

    def one_row(r):
        rs = jnp.clip(r - kh // 2, 0, rows - kh)
        kr = lax.dynamic_slice_in_dim(kg, rs, kh, axis=1)
        vr = lax.dynamic_slice_in_dim(vg, rs, kh, axis=1)
        qr = lax.dynamic_index_in_dim(qg, r, axis=1, keepdims=False)
        row_rel = rs + jnp.arange(kh) - r + NA_ROWS_MAX - 1
        bias = rpb[:, row_rel[:, None, None, None], col_rel[None]]
        bias = bias.transpose(0, 2, 3, 1, 4).astype(jnp.float32)
        s = jnp.einsum("bnqhd,binkhd->bhnqik", qr, kr, preferred_element_type=jnp.float32) * scale + bias[None]
        s = jnp.where(col_ok[None, None, :, :, None, :], s, NEG_INF)
        p = jax.nn.softmax(s.reshape(B, H, ncb, NA_COL_BLOCK, kh * NA_COL_SPAN), axis=-1)
        p = p.reshape(s.shape).astype(v.dtype)
        return jnp.einsum("bhnqik,binkhd->bnqhd", p, vr)

    out = lax.map(one_row, jnp.arange(rows))
    return jnp.moveaxis(out, 0, 1).reshape(B, S, H, dh)


def _token_mixer(h, w_in, rpb, w_branch_a, w_branch_b, w_out):
    B, S, _ = h.shape
    proj = jnp.einsum("bsd,de->bse", h, w_in)
    qa, ka, va, qb, kb, vb, ga, gb = jnp.split(proj, IN_SPLITS, axis=-1)
    pos = jnp.arange(S)
    qa = _partial_rope(_heads(qa, N_HEADS_A), pos)
    ka = _partial_rope(_heads(ka, N_HEADS_A), pos)
    ya = _dilated_mixture_attention(qa, ka, _heads(va, N_HEADS_A)).reshape(B, S, WIDTH_A)
    yb = _neighbourhood_attention(_heads(qb, N_HEADS_B), _heads(kb, N_HEADS_B),
                                  _heads(vb, N_HEADS_B), rpb).reshape(B, S, WIDTH_B)
    ya = jnp.einsum("bse,ed->bsd", ya, w_branch_a)
    yb = jnp.einsum("bse,ed->bsd", yb, w_branch_b)
    merged = jax.nn.sigmoid(ga) * ya + jax.nn.sigmoid(gb) * yb
    return jnp.einsum("bsd,de->bse", merged, w_out)


def _conv_ffn(h, w_up, conv_w, conv_b, w_down):
    S = h.shape[1]
    u = jnp.einsum("bsd,df->bsf", h, w_up)
    pad = CONV_W // 2
    up = jnp.pad(u, ((0, 0), (pad, pad), (0, 0)))
    u = sum(conv_w[t] * up[:, t:t + S] for t in range(CONV_W)) + conv_b
    val, gate = jnp.split(u, 2, axis=-1)
    return jnp.einsum("bsf,fd->bsd", jax.nn.gelu(gate, approximate=True) * val, w_down)


def _trunk(x, c, w_ada, b_ada, g_mix_pre, g_mix_post, g_ffn_pre, g_ffn_post, w_in, rpb,
           w_branch_a, w_branch_b, w_out, w_up, conv_w, conv_b, w_down):
    for l in range(DEPTH):
        mod = jnp.einsum("bd,de->be", jax.nn.silu(c), w_ada[l]) + b_ada[l]
        sh1, sc1, gt1, sh2, sc2, gt2 = jnp.split(mod[:, None, :], 6, axis=-1)
        h = _rms_norm(x, g_mix_pre[l]) * (1 + sc1) + sh1
        x = x + gt1 * _rms_norm(_token_mixer(h, w_in[l], rpb[l], w_branch_a[l], w_branch_b[l], w_out[l]), g_mix_post[l])
        h = _rms_norm(x, g_ffn_pre[l]) * (1 + sc2) + sh2
        x = x + gt2 * _rms_norm(_conv_ffn(h, w_up[l], conv_w[l], conv_b[l], w_down[l]), g_ffn_post[l])
    return x


def setup_inputs(seed: int = 0) -> dict:
    key = jax.random.key(seed)
    ks = jax.random.split(key, 20)
    D, L = D_MODEL, DEPTH

    def nrm(k, shape, scale):
        return jax.random.normal(k, shape, jnp.float32) * scale

    return {
        "x_prompt": nrm(ks[0], (BATCH, SEQ, D), 1.0),
        "x_sample": nrm(ks[1], (DEC_BATCH, DEC_SEQ, D), 1.0),
        "c_prompt": nrm(ks[2], (BATCH, D), 1.0),
        "c_sample": nrm(ks[3], (DEC_BATCH, D), 1.0),
        "w_ada": nrm(ks[4], (L, D, 6 * D), 0.5 * D ** -0.5),
        "b_ada": nrm(ks[5], (L, 6 * D), 0.02),
        "g_mix_pre": 1.0 + nrm(ks[6], (L, D), 0.05),
        "g_mix_post": 1.0 + nrm(ks[7], (L, D), 0.05),
        "g_ffn_pre": 1.0 + nrm(ks[8], (L, D), 0.05),
        "g_ffn_post": 1.0 + nrm(ks[9], (L, D), 0.05),
        "w_in": nrm(ks[10], (L, D, IN_COLS), D ** -0.5),
        "rpb": nrm(ks[11], (L, N_HEADS_B, RPB_ROWS, RPB_COLS), 0.1),
        "w_branch_a": nrm(ks[12], (L, WIDTH_A, D), WIDTH_A ** -0.5),
        "w_branch_b": nrm(ks[13], (L, WIDTH_B, D), WIDTH_B ** -0.5),
        "w_out": nrm(ks[14], (L, D, D), D ** -0.5),
        "w_up": nrm(ks[15], (L, D, 2 * D_FF), D ** -0.5),
        "conv_w": nrm(ks[16], (L, CONV_W, 2 * D_FF), CONV_W ** -0.5),
        "conv_b": nrm(ks[17], (L, 2 * D_FF), 0.02),
        "w_down": nrm(ks[18], (L, D_FF, D), D_FF ** -0.5),
    }


def reference(x_prompt, x_sample, c_prompt, c_sample, w_ada, b_ada, g_mix_pre, g_mix_post,
              g_ffn_pre, g_ffn_post, w_in, rpb, w_branch_a, w_branch_b, w_out, w_up, conv_w,
              conv_b, w_down):
    y_prompt = _trunk(x_prompt, c_prompt, w_ada, b_ada, g_mix_pre, g_mix_post, g_ffn_pre, g_ffn_post,
                      w_in, rpb, w_branch_a, w_branch_b, w_out, w_up, conv_w, conv_b, w_down)
    y_sample = _trunk(x_sample, c_sample, w_ada, b_ada, g_mix_pre, g_mix_post, g_ffn_pre, g_ffn_post,
                      w_in, rpb, w_branch_a, w_branch_b, w_out, w_up, conv_w, conv_b, w_down)
    return (y_prompt, y_sample)
```

```python
import contextlib
import numpy as np
import concourse.bass as bass
import concourse.mybir as mybir
from concourse.bass_utils import run_bass_kernel_spmd

F32 = mybir.dt.float32
BF16 = mybir.dt.bfloat16
AF = mybir.ActivationFunctionType
ALU = mybir.AluOpType

ENGS = ("pe", "act", "dve", "pool", "sp")
NSLOT = 8
S = 2048
D = 1024
NSEQ = 3
NEG = -240000.0
EPS = 1e-6


class Res:
    __slots__ = ("w", "r")

    def __init__(self):
        self.w = None
        self.r = []


class Ins:
    __slots__ = ("eng", "fn", "deps", "sig", "cnt", "dma", "slot", "slotcnt", "qkey")

    def __init__(self, eng, fn, dma):
        self.eng = eng
        self.fn = fn
        self.deps = []
        self.sig = False
        self.cnt = 0
        self.dma = dma
        self.slot = 0
        self.slotcnt = 0
        self.qkey = eng


class Prog:
    def __init__(self, nc):
        self.nc = nc
        self.q = {e: [] for e in ENGS}
        self.ndma = {}
        self.lastdma = {}

    def add(self, eng, fn, reads=(), writes=(), dma=False, group=None):
        x = Ins(eng, fn, dma)
        deps = []
        for r in reads:
            if r.w is not None:
                deps.append(r.w)
        for r in writes:
            if r.w is not None:
                deps.append(r.w)
            deps.extend(r.r)
        for r in reads:
            if not dma:
                r.r = [y for y in r.r if not (y.eng == eng and not y.dma)]
            r.r.append(x)
        for r in writes:
            r.w = x
            r.r = []
        if dma:
            qk = eng if group is None else eng + ":" + group
            x.qkey = qk
            n = self.ndma.get(qk, 0)
            x.slot = n % NSLOT
            x.slotcnt = n // NSLOT + 1
            self.ndma[qk] = n + 1
            prev = self.lastdma.setdefault(qk, {}).get(x.slot)
            if prev is not None:
                deps.append(prev)
            self.lastdma[qk][x.slot] = x
        seen = set()
        for d in deps:
            if d is x or id(d) in seen:
                continue
            seen.add(id(d))
            if (not d.dma) and d.eng == "pe" and eng == "pe" and not dma:
                continue
            x.deps.append(d)
            if not d.dma:
                d.sig = True
        self.q[eng].append(x)
        return x

    def barrier(self):
        lasts = []
        for e in ENGS:
            comp = [y for y in self.q[e] if not y.dma and y.fn is not None]
            if comp:
                comp[-1].sig = True
                lasts.append(comp[-1])
            for _, y in self.lastdma.get(e, {}).items():
                lasts.append(y)
        for e in ENGS:
            x = Ins(e, None, False)
            x.deps = list(lasts)
            self.q[e].append(x)

    def emit(self):
        nc = self.nc
        for e in ENGS:
            c = 0
            for x in self.q[e]:
                if x.dma or x.fn is None:
                    continue
                if x.sig:
                    c += 1
                    x.cnt = c
        with contextlib.ExitStack() as st:
            csem = {e: st.enter_context(nc.semaphore("c_" + e)) for e in ENGS}
            dsem = {qk: [st.enter_context(nc.semaphore("d_%s%d" % (qk.replace(":", "_"), i))) for i in range(NSLOT)]
                    for qk in self.ndma}
            block = st.enter_context(nc.Block())
            engobj = {"pe": "tensor", "act": "scalar", "dve": "vector", "pool": "gpsimd", "sp": "sync"}

            def run(e, eng):
                waited = {}
                for x in self.q[e]:
                    for d in x.deps:
                        if d.dma:
                            key = (d.qkey, d.slot)
                            val = d.slotcnt
                            if waited.get(key, 0) >= val:
                                continue
                            waited[key] = val
                            eng.wait_ge(dsem[d.qkey][d.slot], 16 * val)
                        else:
                            key = d.eng
                            val = d.cnt
                            if waited.get(key, 0) >= val:
                                continue
                            waited[key] = val
                            eng.wait_ge(csem[d.eng], val)
                    if x.fn is None:
                        continue
                    bi = x.fn(eng)
                    if x.dma:
                        bi.then_inc(dsem[x.qkey][x.slot], 16)
                    elif x.sig:
                        bi.then_inc(csem[e], 1)

            for e in ENGS:
                if self.q[e]:
                    getattr(block, engobj[e])(lambda eng, e=e: run(e, eng))


def _rope_tables():
    half = 8
    inv = np.power(np.float32(500000.0), -np.arange(half, dtype=np.float32) / np.float32(half)).astype(np.float32)
    pos = np.arange(S, dtype=np.float32)
    ang = (pos[:, None] * inv[None, :]).astype(np.float32)
    c = np.cos(ang).astype(np.float32).T
    s = np.sin(ang).astype(np.float32).T
    COS = np.ones((128, S), np.float32)
    SIN = np.zeros((128, S), np.float32)
    for b in (0, 64):
        COS[b:b + 8] = c
        COS[b + 8:b + 16] = c
        SIN[b:b + 8] = -s
        SIN[b + 8:b + 16] = s
    perm = np.zeros((128, 128), np.float32)
    for b in (0, 64):
        for i in range(8):
            perm[b + 8 + i, b + i] = 1.0
            perm[b + i, b + 8 + i] = 1.0
    return COS, SIN, perm


def _na_index():
    Rk = np.arange(128) // 64
    ck = np.arange(128) % 64
    rho = np.arange(-6, 8)
    c = np.arange(64)
    row_rel = Rk[:, None, None] - rho[None, :, None] + 7 + 0 * c[None, None, :]
    col_rel = ck[:, None, None] - c[None, None, :] + 15 + 0 * rho[None, :, None]
    ok = (row_rel >= 0) & (row_rel <= 14) & (col_rel >= 0) & (col_rel <= 30)
    cs = np.clip(c - 8, 0, 48)
    colvalid = (ck[:, None, None] >= cs[None, None, :]) & (ck[:, None, None] < cs[None, None, :] + 16) \
        & (rho[None, :, None] > -100)
    d = Rk[:, None, None] - rho[None, :, None] + 0 * c[None, None, :]
    rowvalid = (d >= -4) & (d <= 3)
    m1 = (ok & colvalid & rowvalid).reshape(128, 896)
    m2 = (ok & colvalid).reshape(128, 896)
    return np.clip(row_rel, 0, 14).reshape(128, 896), np.clip(col_rel, 0, 30).reshape(128, 896), m1, m2


C_ID = 0
C_PERM = 128
C_T = 256
C_M1 = 640
C_A1 = 1536
C_M2 = 2432
C_A2 = 3328
C_COS = 4224
C_SIN = 6272
C_ONES = 8320
NCC = 8448

SP_GPRE, SP_GPOST, SP_FPRE, SP_FPOST, SP_BADA, SP_CW0, SP_CW1, SP_CW2, SP_CB, SP_C = 0, 8, 16, 24, 32, 80, 124, 168, 212, 256
NSP = 280


def _fm(v, nch):
    return np.ascontiguousarray(v.reshape(nch, 128).T)


def build(debug=False):
    nc = bass.Bass("TRN2", target_bir_lowering=False)
    nseq = 1 if debug else NSEQ
    dt = lambda name, shape, kind="ExternalInput": nc.dram_tensor(name, shape, F32, kind=kind)
    x_d = dt("x", [NSEQ, S, D])
    y_d = dt("y", [NSEQ, S, D], "ExternalOutput")
    consts_d = dt("consts", [128, NCC])
    sp_d = dt("sp", [128, NSP])
    G_d = dt("G", [8, 128, 896])
    wada_d = dt("w_ada", [D, 6 * D])
    win_d = dt("w_in", [D, 5120])
    wba_d = dt("w_ba", [512, D])
    wbb_d = dt("w_bb", [512, D])
    wout_d = dt("w_out", [D, D])
    wup_d = dt("w_up", [D, 5632])
    wdn_d = dt("w_down", [2816, D])

    P = Prog(nc)
    A = nc.alloc_sbuf_tensor
    if debug:
        dbg = {
            "hT": nc.dram_tensor("dbg_hT", [128, 8 * S], BF16, kind="ExternalOutput"),
            "h2T": nc.dram_tensor("dbg_h2T", [128, 8 * S], BF16, kind="ExternalOutput"),
            "attA": nc.dram_tensor("dbg_attA", [128, 4 * S], BF16, kind="ExternalOutput"),
            "attB": nc.dram_tensor("dbg_attB", [128, 4 * S], BF16, kind="ExternalOutput"),
            "x1": nc.dram_tensor("dbg_x1", [128, 16 * D], F32, kind="ExternalOutput"),
            "modT": nc.dram_tensor("dbg_modT", [128, 144], F32, kind="ExternalOutput"),
            "gt1bc": nc.dram_tensor("dbg_gt1bc", [128, D], F32, kind="ExternalOutput"),
            "QK": nc.dram_tensor("dbg_QK", [128, 4096], BF16, kind="ExternalOutput"),
            "V": nc.dram_tensor("dbg_V", [128, 12288], BF16, kind="ExternalOutput"),
        }

    def dump(name, src):
        if debug:
            P.barrier()
            P.add("sp", lambda e: e.dma_start(out=dbg[name].ap(), in_=src), dma=True)
            P.barrier()

    identf = A("identf", [128, 128], F32)
    onesf = A("onesf", [128, 128], F32)
    identb = A("identb", [128, 128], BF16)
    permb = A("permb", [128, 128], BF16)
    Tb = A("Tb", [128, 384], BF16)
    csb = A("csb", [128, 2 * S], BF16)
    cosb, sinb = csb[:, 0:S], csb[:, S:2 * S]
    spt = A("spt", [128, NSP], F32)
    modT = A("modT", [128, 144], F32)
    vec = A("vec", [128, 64], F32)
    gt1bc = A("gt1bc", [128, D], F32)
    gt2bc = A("gt2bc", [128, D], F32)
    stat = A("stat", [128, 64], F32)
    NW = 2
    wring = [A("wr%d" % i, [128, 8 * 512], BF16) for i in range(NW)]
    hT = A("hT", [128, 8 * S], BF16)
    attA = A("attA", [128, 4 * S], BF16)
    attB = A("attB", [128, 4 * S], BF16)
    RA = A("RA", [128, 16 * D], F32)
    RAb = RA.bitcast(BF16)
    VA_OFF = 0
    QT_OFF = 12288
    KT_OFF = 14336
    QRAW_OFF = 16384
    EXP_OFF = 17408
    QO_OFF = 30720
    ACC_OFF_F = 10240
    RDEN_OFF_F = 12288
    TMP_OFF_F = 14336
    RC = A("RC", [128, 9728], F32)
    RCb = RC.bitcast(BF16)
    RB = RC[:, 5632:9728]
    gT = RCb[:, 0:11264]
    badd = RCb[:, 0:14336]
    attAf = attA.bitcast(F32)
    t1v, t1g, sgs = attAf[:, 0:512], attAf[:, 512:1024], attAf[:, 1024:2048]
    t1v_c, t1g_c, sgs_c = RB[:, 0:512], RB[:, 512:1024], RB[:, 1024:2048]
    h2halo = A("h2halo", [128, 8 * 8], BF16)
    Etab = A("Etab", [128, 44 * 6], F32)
    junk = A("junk", [128, D], BF16)

    pb = [nc.alloc_psum_tensor("pb%d" % i, [128, 512], F32) for i in range(8)]
    rb = [Res() for _ in range(8)]
    bank_i = [0]
    ffb = [0]
    ffs = [0]
    pend_epi = [None]

    def nb():
        i = bank_i[0] % 8
        bank_i[0] += 1
        return pb[i], rb[i]

    def ap(t, rowlen, p0, np_, off, dims):
        return bass.AP(t, p0 * rowlen + off, [[rowlen, np_]] + [list(d) for d in dims])

    HT_L, ATT_L, RAB_L, RAF_L = 8 * S, 4 * S, 32768, 16 * D

    r_const = Res()
    r_t1v, r_t1g, r_sg0, r_sg1, r_halo, r_ta_p, r_tb_p, r_stln, r_badd = [Res() for _ in range(9)]
    r_qraw = [Res(), Res()]
    r_xb = [Res() for _ in range(4)]
    r_x1all, r_xn2, r_mgp = Res(), Res(), Res()
    r_Qp, r_Kp, r_Vp, r_accp, r_rdp = [Res() for _ in range(5)]
    r_pc = [(Res(), Res()), (Res(), Res())]
    r_ff = [(Res(), Res()), (Res(), Res()), (Res(), Res())]
    r_exb = [Res() for _ in range(4)]
    r_hT = [[Res() for _ in range(4)] for _ in range(8)]
    class Slot:
        def __init__(self, t, rowlen, off):
            self.t, self.rowlen, self.off, self.res = t, rowlen, off, Res()

    class Ring:
        def __init__(self, slots):
            self.slots, self.i = slots, 0

        def next(self):
            sl = self.slots[self.i % len(self.slots)]
            self.i += 1
            return sl

    sl_w0, sl_w1 = Slot(wring[0], 4096, 0), Slot(wring[1], 4096, 0)
    ring0 = Ring([sl_w0, sl_w1])
    sl_cs = Slot(csb, 2 * S, 0)
    r_cs = sl_cs.res
    ringC = Ring([sl_w0, sl_w1, sl_cs])
    ringAB = Ring([Slot(RCb, 19456, 4096), Slot(RCb, 19456, 15360)])
    ringF = Ring([sl_w0, sl_w1, Slot(attB, ATT_L, 0), Slot(attB, ATT_L, 4096)])

    def wload(srcd, kc, ncols, ring=None):
        src, deps = srcd
        sl = (ring or ring0).next()
        dst = bass.AP(sl.t, sl.off, [[sl.rowlen, 128], [ncols, kc], [1, ncols]])
        q_ = "sp" if deps else "pool"
        P.add(q_, lambda e, dst=dst, src=src: e.dma_start(out=dst, in_=src), reads=deps, writes=[sl.res], dma=True)
        return sl, sl.res

    def wsl(sl, k, ncols, c0, n):
        return bass.AP(sl.t, sl.off + k * ncols + c0, [[sl.rowlen, 128], [1, n]])

    wconv = {}

    cv_pending = []

    def convert(w_d, nrows, ncols, name):
        wb_d = nc.dram_tensor(name + "_bf", [nrows, ncols], BF16, kind="Internal")
        rs = []
        for c0 in range(0, ncols, 512):
            r = Res()
            cv_pending.append(lambda c0=c0, r=r, wb_d=wb_d, w_d=w_d: P.add(
                "pool", lambda e: e.dma_start(out=wb_d.ap()[:, c0:c0 + 512], in_=w_d.ap()[:, c0:c0 + 512]), writes=[r], dma=True, group="cv"))
            rs.append(r)
        wconv[id(w_d)] = (wb_d, rs)

    def convert_step(n):
        for _ in range(n):
            if cv_pending:
                cv_pending.pop(0)()

    def wsrc(w_d, nrows_k, c0, ncols, k0=0, kc=None, rowlen=None):
        kc = nrows_k if kc is None else kc
        deps = []
        if id(w_d) in wconv:
            w_d, rs = wconv[id(w_d)]
            deps = rs[c0 // 512:(c0 + ncols - 1) // 512 + 1]
        return bass.AP(w_d, (k0 * 128) * rowlen + c0, [[rowlen, 128], [128 * rowlen, kc], [1, ncols]]), deps

    P.add("sp", lambda e: e.dma_start(out=identf[:], in_=consts_d.ap()[:, C_ID:C_ID + 128]), writes=[r_const], dma=True)
    P.add("sp", lambda e: e.dma_start(out=onesf[:], in_=consts_d.ap()[:, C_ONES:C_ONES + 128]), writes=[r_const], dma=True)
    P.add("sp", lambda e: e.dma_start(out=spt[:], in_=sp_d.ap()), writes=[r_const], dma=True)
    for (t, c0, n) in ((identb, C_ID, 128), (permb, C_PERM, 128), (Tb, C_T, 384)):
        P.add("pool", lambda e, t=t, c0=c0, n=n: e.dma_start(out=t[:], in_=consts_d.ap()[:, c0:c0 + n]),
              writes=[r_const], dma=True)
    P.add("dve", lambda e: e.memset(RAb[:, 0:12288], 1.0), writes=[r_const])
    P.barrier()
    attBf = attB.bitcast(F32)
    r_gst = [Res() for _ in range(4)]
    r_dg_cur = [None]
    r_cap = Res()

    def gen_badd_part(part):
        if part == 0:
            P.add("sp", lambda e: e.dma_start(out=attAf[:, 0:1792], in_=consts_d.ap()[:, C_M1:C_M1 + 1792]), writes=[r_cap], dma=True)
        for h in (2 * part, 2 * part + 1):
            g = attBf[:, (h % 4) * 1024:(h % 4) * 1024 + 896]
            rg = r_gst[h % 4]
            P.add("sp", lambda e, g=g, h=h: e.dma_start(out=g, in_=G_d.ap()[h]), writes=[rg], dma=True)
            for v in range(2):
                o = badd[:, (h * 2 + v) * 896:(h * 2 + v + 1) * 896]
                P.add("dve", lambda e, o=o, g=g, v=v: e.scalar_tensor_tensor(out=o, in0=g, scalar=8.0, in1=attAf[:, v * 896:(v + 1) * 896],
                                                                          op0=ALU.mult, op1=ALU.min), reads=[rg, r_cap], writes=[r_badd, r_dg_cur[0]])

    silub = A("silub", [128, 24], BF16)
    r_sil = Res()
    P.add("act", lambda e: e.activation(out=silub[:], in_=spt[:, SP_C:SP_C + 24], func=AF.Silu), writes=[r_sil])
    modps, r_modps = pb[0], rb[0]
    for tile in range(12):
        wt, wr_ = wload(wsrc(wada_d, 8, tile * 512, 512, rowlen=6 * D), 8, 512)
        for sub in range(4):
            ch = tile * 4 + sub
            for k in range(8):
                P.add("pe", lambda e, wt=wt, sub=sub, k=k, ch=ch: e.matmul(
                    modps[:, ch * 3:ch * 3 + 3], lhsT=wsl(wt, k, 512, sub * 128, 128),
                    rhs=bass.AP(silub, k * 3, [[24, 128], [1, 3]]), start=(k == 0), stop=(k == 7), skip_group_check=True),
                    reads=[wr_, r_sil], writes=[r_modps])
    for (w_d_, nr_, ncl_, nm_) in ((win_d, D, 5120, "w_in"), (wba_d, 512, D, "w_ba"), (wbb_d, 512, D, "w_bb"), (wout_d, D, D, "w_out"),
                                   (wup_d, D, 5632, "w_up"), (wdn_d, 2816, D, "w_down")):
        convert(w_d_, nr_, ncl_, nm_)
    convert_step(3)
    r_mod = Res()
    P.add("dve", lambda e: e.tensor_tensor(
        out=bass.AP(modT, 0, [[144, 128], [3, 48], [1, 3]]), in0=bass.AP(modps, 0, [[512, 128], [3, 48], [1, 3]]),
        in1=bass.AP(spt, SP_BADA, [[NSP, 128], [1, 48], [0, 3]]), op=ALU.add), reads=[r_modps, r_const], writes=[r_mod])
    P.barrier()

    def modv(part, s):
        return bass.AP(modT, part * 24 + s, [[144, 128], [3, 8]])

    r_lnst = [(Res(), Res(), Res()), (Res(), Res(), Res())]
    _rst[0] = [Res() for _ in range(5)]
    r_junk_all = _rst[0][4]

    def ln_stats(src_tiles, src_res, tcn):
        r_ss, r_rs, r_rs2 = r_lnst[tcn % 2]
        so = 4 * (tcn % 2)
        for tt in range(4):
            P.add("act", lambda e, tt=tt: e.activation(out=junk[:], in_=src_tiles[tt], func=AF.Square,
                                                      accum_out=stat[:, so + tt:so + tt + 1]), reads=[src_res], writes=[r_ss, r_junk_all])
        P.add("dve", lambda e: e.tensor_scalar(out=stat[:, 8 + so:12 + so], in0=stat[:, so:so + 4], scalar1=1.0 / D, scalar2=EPS,
                                               op0=ALU.mult, op1=ALU.add), reads=[r_ss], writes=[r_rs])
        P.add("pool", lambda e: e.tensor_tensor(out=stat[:, 16 + so:20 + so], in0=stat[:, 8 + so:12 + so], in1=mhalf[:, 0:4], op=ALU.pow),
              reads=[r_rs, r_const], writes=[r_rs2])

    def ln_apply(src_tiles, src_res, tcn, sc_ap, sh_ap, inplace_dst, r_xn):
        r_ss, r_rs, r_rs2 = r_lnst[tcn % 2]
        so = 4 * (tcn % 2)
        for tt in range(4):
            P.add("dve", lambda e, tt=tt: e.tensor_scalar(out=inplace_dst[tt], in0=src_tiles[tt], scalar1=stat[:, 16 + so + tt:17 + so + tt],
                                                          scalar2=None, op0=ALU.mult), reads=[src_res, r_rs2], writes=[r_xn])
        for j in range(8):
            ps, rps = nb()
            for tt in range(4):
                P.add("pe", lambda e, ps=ps, tt=tt, j=j: e.transpose(out=ps[:, tt * 128:(tt + 1) * 128],
                                                                     in_=inplace_dst[tt][:, j * 128:(j + 1) * 128], identity=identf[:]),
                      reads=[r_xn, r_const], writes=[rps])
            o = hT[:, j * S + tcn * 512: j * S + tcn * 512 + 512]
            if j % 2 == 0:
                P.add("act", lambda e, o=o, ps=ps, j=j: e.activation(out=o, in_=ps[:], func=AF.Identity, scale=sc_ap[:, j:j + 1],
                                                                     bias=sh_ap[:, j:j + 1]), reads=[rps, r_vec], writes=[r_hT[j][tcn]])
            else:
                P.add("dve", lambda e, o=o, ps=ps, j=j: e.tensor_scalar(out=o, in0=ps[:], scalar1=sc_ap[:, j:j + 1], scalar2=sh_ap[:, j:j + 1],
                                                                        op0=ALU.mult, op1=ALU.add), reads=[rps, r_vec], writes=[r_hT[j][tcn]])

    epsb = A("epsb", [128, 1], F32)
    mhalf = A("mhalf", [128, 4], F32)
    P.add("dve", lambda e: e.memset(epsb[:], EPS), writes=[r_const])
    P.add("dve", lambda e: e.memset(mhalf[:], -0.5), writes=[r_const])
    r_vec = Res()
    hT_all = [r_hT[j][t] for j in range(8) for t in range(4)]

    def x1tile(i):
        return RA[:, i * D:(i + 1) * D]

    def xbuf(tcn):
        return [bass.AP(RA, tcn * 4096 + tt * 1024, [[RAF_L, 128], [1, 1024]]) for tt in range(4)]

    for s in range(nseq):
        for tcn in range(4):
            dstx = bass.AP(RA, tcn * 4096, [[RAF_L, 128], [1024, 4], [1, 1024]])
            srcx = bass.AP(x_d, (s * S + tcn * 512) * D, [[D, 128], [128 * D, 4], [1, D]])
            P.add("sp", lambda e, dstx=dstx, srcx=srcx: e.dma_start(out=dstx, in_=srcx), writes=[r_xb[tcn]], dma=True)
        P.add("dve", lambda e, s=s: e.scalar_tensor_tensor(out=vec[:, 0:8], in0=modv(1, s), scalar=1.0, in1=spt[:, SP_GPRE:SP_GPRE + 8],
                                                           op0=ALU.add, op1=ALU.mult), reads=[r_mod, r_const], writes=[r_vec])
        P.add("dve", lambda e, s=s: e.tensor_copy(out=vec[:, 8:16], in_=modv(0, s)), reads=[r_mod], writes=[r_vec])
        P.add("dve", lambda e, s=s: e.tensor_tensor(out=vec[:, 16:24], in0=modv(2, s), in1=spt[:, SP_GPOST:SP_GPOST + 8], op=ALU.mult),
              reads=[r_mod, r_const], writes=[r_vec])
        P.add("dve", lambda e, s=s: e.scalar_tensor_tensor(out=vec[:, 24:32], in0=modv(4, s), scalar=1.0, in1=spt[:, SP_FPRE:SP_FPRE + 8],
                                                           op0=ALU.add, op1=ALU.mult), reads=[r_mod, r_const], writes=[r_vec])
        P.add("dve", lambda e, s=s: e.tensor_copy(out=vec[:, 32:40], in_=modv(3, s)), reads=[r_mod], writes=[r_vec])
        P.add("dve", lambda e, s=s: e.tensor_tensor(out=vec[:, 40:48], in0=modv(5, s), in1=spt[:, SP_FPOST:SP_FPOST + 8], op=ALU.mult),
              reads=[r_mod, r_const], writes=[r_vec])
        r_bc = Res()
        r_dg = Res()
        r_dg_cur[0] = r_dg
        for (c0, dst) in ((16, gt1bc), (40, gt2bc)):
            dg = RB
            P.add("dve", lambda e, c0=c0: e.tensor_tensor(
                out=bass.AP(RC, 5632, [[9728, 128], [128, 8], [1, 128]]), in0=bass.AP(identf, 0, [[128, 128], [0, 8], [1, 128]]),
                in1=bass.AP(vec, c0, [[64, 128], [1, 8], [0, 128]]), op=ALU.mult), reads=[r_vec, r_const], writes=[r_dg])
            for hf in range(2):
                ps, rps = nb()
                P.add("pe", lambda e, ps=ps, hf=hf: e.matmul(ps[:], lhsT=onesf[:], rhs=RB[:, hf * 512:(hf + 1) * 512], start=True, stop=True),
                      reads=[r_dg, r_const], writes=[rps])
                P.add("act", lambda e, ps=ps, hf=hf, dst=dst: e.activation(out=dst[:, hf * 512:(hf + 1) * 512], in_=ps[:], func=AF.Identity),
                      reads=[rps], writes=[r_bc])

        ln_stats(xbuf(0), r_xb[0], 0)
        for tcn in range(4):
            if tcn + 1 < 4:
                ln_stats(xbuf(tcn + 1), r_xb[tcn + 1], tcn + 1)
            ln_apply(xbuf(tcn), r_xb[tcn], tcn, vec[:, 0:8], vec[:, 8:16], xbuf(tcn), r_xb[tcn])
            gen_badd_part(tcn)
        P.barrier()
        dump('hT', hT[:])
        dump('modT', modT[:])
        dump('gt1bc', gt1bc[:])
        P.add("dve", lambda e: e.memset(RAb[:, 0:12288], 1.0), writes=[r_const])
        P.add("dve", lambda e: e.memset(bass.AP(RAb, 64 * RAB_L + QT_OFF, [[RAB_L, 64], [1, S]]), 0.0), writes=[r_const])
        P.add("dve", lambda e: e.memset(bass.AP(RAb, QO_OFF, [[RAB_L, 64], [1, S]]), 0.0), writes=[r_const])
        P.barrier()

        P.add("pool", lambda e: e.dma_start(out=csb[:], in_=consts_d.ap()[:, C_COS:C_COS + 2 * S]), writes=[r_cs], dma=True)
        for mixer in range(2):
            if mixer == 1:
                dump('QK', RAb[:, QT_OFF:QT_OFF + 4096])
                dump('V', RAb[:, 0:12288])
            att = attA if mixer == 0 else attB
            qcol0 = 0 if mixer == 0 else 1536
            kcol0 = 512 if mixer == 0 else 2048
            vcol0 = 1024 if mixer == 0 else 2560
            norders = 3 if mixer == 0 else 1
            for hp in range(4):
                wt, wr_ = wload(wsrc(win_d, 8, vcol0 + hp * 128, 128, rowlen=5120), 8, 128)
                r_V = r_Vp
                for tcn in range(4):
                    ps, rps = nb()
                    for k in range(8):
                        P.add("pe", lambda e, ps=ps, k=k, tcn=tcn, wt=wt: e.matmul(
                            ps[:], lhsT=wsl(wt, k, 128, 0, 128), rhs=hT[:, k * S + tcn * 512:k * S + tcn * 512 + 512],
                            start=(k == 0), stop=(k == 7)), reads=hT_all + [wr_], writes=[rps])
                    P.add("act", lambda e, ps=ps, tcn=tcn: e.activation(out=RAb[:, EXP_OFF + tcn * 512:EXP_OFF + tcn * 512 + 512], in_=ps[:], func=AF.Identity),
                          reads=[rps], writes=[r_exb[tcn]])
                for o in range(norders):
                    for g4 in range(4):
                        ps, rps = nb()
                        psb16 = ps.bitcast(BF16)
                        for q in range(4):
                            i = g4 * 4 + q
                            if o == 0:
                                off, st_ = 128 * i, 1
                            elif o == 1:
                                off, st_ = 512 * (i % 4) + i // 4, 4
                            else:
                                off, st_ = i, 16
                            P.add("pe", lambda e, psb16=psb16, q=q, off=off, st_=st_: e.transpose(
                                out=psb16[:, q * 128:(q + 1) * 128], in_=bass.AP(RAb, EXP_OFF + off, [[RAB_L, 128], [st_, 128]]), identity=identb[:]),
                                reads=r_exb + [r_const], writes=[rps])
                        dst = bass.AP(RAb, VA_OFF + (o * 16 + g4 * 4) * 256, [[RAB_L, 128], [256, 4], [192, 2], [1, 64]])
                        srcp = bass.AP(psb16, 0, [[1024, 128], [128, 4], [64, 2], [1, 64]])
                        eng_ = "act" if g4 % 2 == 0 else "dve"
                        if eng_ == "act":
                            P.add("act", lambda e, dst=dst, srcp=srcp: e.activation(out=dst, in_=srcp, func=AF.Identity), reads=[rps], writes=[r_V])
                        else:
                            P.add("dve", lambda e, dst=dst, srcp=srcp: e.tensor_copy(out=dst, in_=srcp), reads=[rps], writes=[r_V])
                r_Q, r_K = r_Qp, r_Kp
                convert_step(3)
                for (col0, toff, rres) in ((qcol0, QT_OFF, r_Q), (kcol0, KT_OFF, r_K)):
                    wt, wr_ = wload(wsrc(win_d, 8, col0 + hp * 128, 128, rowlen=5120), 8, 128)
                    for tcn in range(4):
                        ps, rps = nb()
                        for k in range(8):
                            P.add("pe", lambda e, ps=ps, k=k, tcn=tcn, wt=wt: e.matmul(
                                ps[:], lhsT=wsl(wt, k, 128, 0, 128), rhs=hT[:, k * S + tcn * 512:k * S + tcn * 512 + 512],
                                start=(k == 0), stop=(k == 7)), reads=hT_all + [wr_], writes=[rps])
                        dstq = RAb[:, toff + tcn * 512: toff + tcn * 512 + 512]
                        isq = (toff == QT_OFF)
                        dq_e = bass.AP(RAb, QT_OFF + tcn * 512, [[RAB_L, 64], [1, 512]])
                        dq_o = bass.AP(RAb, 64 * RAB_L + QO_OFF + tcn * 512, [[RAB_L, 64], [1, 512]])
                        if mixer == 1 and isq:
                            P.add("act", lambda e, dq_e=dq_e, ps=ps: e.activation(out=dq_e, in_=ps[0:64, :], func=AF.Identity),
                                  reads=[rps], writes=[rres])
                            P.add("act", lambda e, dq_o=dq_o, ps=ps: e.activation(out=dq_o, in_=ps[64:128, :], func=AF.Identity),
                                  reads=[rps], writes=[rres])
                        elif mixer == 1:
                            P.add("act", lambda e, dstq=dstq, ps=ps: e.activation(out=dstq, in_=ps[:], func=AF.Identity),
                                  reads=[rps], writes=[rres])
                        else:
                            qraw = RAb[:, QRAW_OFF + (tcn % 2) * 512: QRAW_OFF + (tcn % 2) * 512 + 512]
                            r_qr = r_qraw[tcn % 2]
                            P.add("act", lambda e, qraw=qraw, ps=ps: e.activation(out=qraw, in_=ps[:], func=AF.Identity),
                                  reads=[rps], writes=[r_qr])
                            ps2, rps2 = nb()
                            P.add("pe", lambda e, ps2=ps2, qraw=qraw: e.matmul(ps2[:], lhsT=permb[:], rhs=qraw, start=True, stop=True),
                                  reads=[r_qr, r_const], writes=[rps2])
                            ta = RA[:, TMP_OFF_F: TMP_OFF_F + 512]
                            tb_ = RA[:, TMP_OFF_F + 512: TMP_OFF_F + 1024]
                            r_ta, r_tb = r_ta_p, r_tb_p
                            P.add("dve", lambda e, ta=ta, ps2=ps2, tcn=tcn: e.tensor_tensor(out=ta, in0=ps2[:], in1=sinb[:, tcn * 512:(tcn + 1) * 512],
                                                                                          op=ALU.mult), reads=[rps2, r_cs], writes=[r_ta])
                            P.add("dve", lambda e, tb_=tb_, qraw=qraw, tcn=tcn: e.tensor_tensor(out=tb_, in0=qraw, in1=cosb[:, tcn * 512:(tcn + 1) * 512],
                                                                                              op=ALU.mult), reads=[r_qr, r_cs], writes=[r_tb])
                            if isq:
                                P.add("dve", lambda e, dq_e=dq_e, ta=ta, tb_=tb_: e.tensor_tensor(out=dq_e, in0=ta[0:64, :], in1=tb_[0:64, :], op=ALU.add),
                                      reads=[r_ta, r_tb], writes=[rres])
                                P.add("dve", lambda e, dq_o=dq_o, ta=ta, tb_=tb_: e.tensor_tensor(out=dq_o, in0=ta[64:128, :], in1=tb_[64:128, :], op=ALU.add),
                                      reads=[r_ta, r_tb], writes=[rres])
                            else:
                                P.add("dve", lambda e, dstq=dstq, ta=ta, tb_=tb_: e.tensor_tensor(out=dstq, in0=ta, in1=tb_, op=ALU.add),
                                      reads=[r_ta, r_tb], writes=[rres])
                for hl in range(2):
                    h = hp * 2 + hl
                    b = 64 * hl
                    accb = [pb[i] for i in range(4)]
                    accr = [rb[i] for i in range(4)]
                    scb = [(pb[4 + i], rb[4 + i]) for i in range(4)]
                    sci = [0]
                    r_acc = r_accp
                    acc_sb = lambda c0, st_, n, p0=0, np_=128: bass.AP(RA, p0 * RAF_L + ACC_OFF_F + c0, [[RAF_L, np_], [st_, n]])

                    LOOK = 2
                    pipe = []

                    def qk_part(units):
                        slot = sci[0] % 4
                        sci[0] += 1
                        ps, rps = scb[slot]
                        col = 0
                        for (koff, kst, vtile, segs) in units:
                            lk = bass.AP(RAb, KT_OFF + koff, [[RAB_L, 128], [kst, 128]])
                            for (qoff, qst, nq, mrhs, bk, ac0) in segs:
                                rq = bass.AP(RAb, (QT_OFF if hl == 0 else QO_OFF) + qoff, [[RAB_L, 128], [qst, nq]])
                                P.add("pe", lambda e, ps=ps, col=col, nq=nq, lk=lk, rq=rq: e.matmul(
                                    ps[:, col:col + nq], lhsT=lk, rhs=rq, start=True, stop=False, skip_group_check=True),
                                    reads=[r_Q, r_K], writes=[rps])
                                P.add("pe", lambda e, ps=ps, col=col, nq=nq, mrhs=mrhs: e.matmul(
                                    ps[:, col:col + nq], lhsT=identb[:], rhs=mrhs, start=False, stop=True, skip_group_check=True),
                                    reads=[r_const, r_badd], writes=[rps])
                                col += nq
                        ex = RAb[:, EXP_OFF + slot * 512: EXP_OFF + slot * 512 + col]
                        P.add("act", lambda e, ex=ex, ps=ps, col=col: e.activation(out=ex, in_=ps[:, 0:col], func=AF.Exp, scale=0.125),
                              reads=[rps], writes=[r_exb[slot]])
                        return (slot, units)

                    def pv_part(ctx, touched):
                        slot, units = ctx
                        col = 0
                        for (koff, kst, vtile, segs) in units:
                            lv = bass.AP(RAb, VA_OFF + vtile * 256 + hl * 128, [[RAB_L, 128], [1, 128]])
                            for (qoff, qst, nq, mrhs, bk, ac0) in segs:
                                first = bk not in touched
                                touched.add(bk)
                                exs = bass.AP(RAb, EXP_OFF + slot * 512 + col, [[RAB_L, 128], [1, nq]])
                                P.add("pe", lambda e, bk=bk, ac0=ac0, nq=nq, lv=lv, exs=exs, first=first: e.matmul(
                                    accb[bk][:, ac0:ac0 + nq], lhsT=lv, rhs=exs, start=first, stop=True, skip_group_check=True),
                                    reads=[r_exb[slot], r_V], writes=[accr[bk]])
                                col += nq

                    def push(units, touched, after=None):
                        pipe.append((qk_part(units), touched, after))
                        if len(pipe) > LOOK:
                            c_, t_, a_ = pipe.pop(0)
                            pv_part(c_, t_)
                            if a_ is not None:
                                a_()

                    def flush():
                        while pipe:
                            c_, t_, a_ = pipe.pop(0)
                            pv_part(c_, t_)
                            if a_ is not None:
                                a_()

                    den_p0 = 64 - b
                    if mixer == 0:
                        def merge(o):
                            for bk in range(4):
                                if o == 0:
                                    dsta = acc_sb(bk * 512, 1, 512)
                                    P.add("act", lambda e, dsta=dsta, bk=bk: e.activation(out=dsta, in_=accb[bk][:], func=AF.Identity),
                                          reads=[accr[bk]], writes=[r_acc])
                                elif o == 1:
                                    dsta = acc_sb(bk, 4, 512)
                                    P.add("dve", lambda e, dsta=dsta, bk=bk: e.tensor_tensor(out=dsta, in0=accb[bk][:], in1=dsta, op=ALU.add),
                                          reads=[accr[bk], r_acc], writes=[r_acc])
                                else:
                                    dsta = bass.AP(RA, ACC_OFF_F + 4 * bk, [[RAF_L, 128], [1, 4], [16, 128]])
                                    srca = bass.AP(accb[bk], 0, [[512, 128], [128, 4], [1, 128]])
                                    P.add("dve", lambda e, dsta=dsta, srca=srca: e.tensor_tensor(out=dsta, in0=srca, in1=dsta, op=ALU.add),
                                          reads=[accr[bk], r_acc], writes=[r_acc])
                            if o == 2:
                                rden = bass.AP(RA, b * RAF_L + RDEN_OFF_F, [[RAF_L, 64], [1, S]])
                                r_rd = r_rdp
                                P.add("act", lambda e, rden=rden, den_p0=den_p0: e.activation(out=rden, in_=acc_sb(0, 1, S, den_p0, 64), func=AF.Ln),
                                      reads=[r_acc], writes=[r_rd])
                                P.add("act", lambda e, rden=rden: e.activation(out=rden, in_=rden, func=AF.Exp, scale=-1.0), writes=[r_rd])
                                dsto = bass.AP(att, b * ATT_L + hp * S, [[ATT_L, 64], [1, S]])
                                P.add("dve", lambda e, dsto=dsto, rden=rden, b=b: e.tensor_tensor(out=dsto, in0=acc_sb(0, 1, S, b, 64), in1=rden, op=ALU.mult),
                                      reads=[r_acc, r_rd], writes=[Res()])

                        for o, d in enumerate((1, 4, 16)):
                            L = S // d
                            touched = set()
                            ulist = []
                            for r in range(d):
                                for blk in range(L // 128):
                                    j0 = 128 * blk
                                    jq0, jq1 = max(0, j0 - 64), min(L, j0 + 192)
                                    segs = []
                                    pos = r * L + jq0
                                    end = r * L + jq1
                                    while pos < end:
                                        e_ = min(end, (pos // 512 + 1) * 512)
                                        jj = pos - r * L
                                        x0 = jj - j0 + 64
                                        segs.append((d * jj + r, d, e_ - pos, Tb[:, x0:x0 + (e_ - pos)], pos // 512, pos % 512))
                                        pos = e_
                                    ulist.append((d * j0 + r, d, o * 16 + (r * L + j0) // 128, segs))
                            groups = [ulist[i:i + 2] for i in range(0, len(ulist), 2)]
                            for gi, units in enumerate(groups):
                                push(units, touched, (lambda o=o: merge(o)) if gi == len(groups) - 1 else None)
                    else:
                        def norm_b():
                            for bk in range(4):
                                rden = bass.AP(RA, b * RAF_L + RDEN_OFF_F + bk * 512, [[RAF_L, 64], [1, 512]])
                                r_rd = r_rdp
                                P.add("act", lambda e, rden=rden, bk=bk, den_p0=den_p0: e.activation(out=rden, in_=accb[bk][den_p0:den_p0 + 64, :], func=AF.Ln),
                                      reads=[accr[bk]], writes=[r_rd])
                                P.add("act", lambda e, rden=rden: e.activation(out=rden, in_=rden, func=AF.Exp, scale=-1.0), writes=[r_rd])
                                dsto = bass.AP(att, b * ATT_L + hp * S + bk * 512, [[ATT_L, 64], [1, 512]])
                                P.add("dve", lambda e, dsto=dsto, rden=rden, bk=bk, b=b: e.tensor_tensor(out=dsto, in0=accb[bk][b:b + 64, :], in1=rden, op=ALU.mult),
                                      reads=[accr[bk], r_rd], writes=[Res()])

                        touched = set()
                        ulist = []
                        for m in range(16):
                            rp_lo, rp_hi = max(4, 2 * m - 3), min(28, 2 * m + 5)
                            r_lo = 0 if rp_lo == 4 else rp_lo
                            r_hi = 31 if rp_hi == 28 else rp_hi
                            r0 = r_lo
                            while r0 <= r_hi:
                                r1 = min(r_hi, (r0 // 8) * 8 + 7)
                                segs = []
                                rr = r0
                                while rr <= r1:
                                    var = 0 if 4 <= rr <= 28 else 1
                                    re_ = rr
                                    while re_ + 1 <= r1 and (0 if 4 <= re_ + 1 <= 28 else 1) == var:
                                        re_ += 1
                                    nq = 64 * (re_ - rr + 1)
                                    rho0 = rr - 2 * m
                                    tb0 = (h * 2 + var) * 896 + (rho0 + 6) * 64
                                    segs.append((64 * rr, 1, nq, badd[:, tb0:tb0 + nq], rr // 8, (64 * rr) % 512))
                                    rr = re_ + 1
                                ulist.append([(128 * m, 1, m, segs)])
                                r0 = r1 + 1
                        for gi, units in enumerate(ulist):
                            push(units, touched, norm_b if gi == len(ulist) - 1 else None)
                    flush()

        convert_step(100)
        P.barrier()
        dump('attA', attA[:])
        dump('attB', attB[:])
        for tcn in range(4):
            r_x = Res()
            dstx = bass.AP(RA, tcn * 4096, [[RAF_L, 128], [1024, 4], [1, 1024]])
            srcx = bass.AP(x_d, (s * S + tcn * 512) * D, [[D, 128], [128 * D, 4], [1, D]])
            P.add("sp", lambda e, dstx=dstx, srcx=srcx: e.dma_start(out=dstx, in_=srcx), writes=[r_x], dma=True)
            r_mg = r_mgp
            pend_add = None
            if tcn == 0:
                wa, wra = wload(wsrc(wba_d, 4, 0, 1024, rowlen=D), 4, 1024, ringAB)
                wb_, wrb = wload(wsrc(wbb_d, 4, 0, 1024, rowlen=D), 4, 1024, ringAB)
            for c in range(8):
                psa, rpsa = nb()
                psb, rpsb = nb()
                for (ps_, w_, wr__, at_) in ((psa, wa, wra, attA), (psb, wb_, wrb, attB)):
                    for k in range(4):
                        P.add("pe", lambda e, ps_=ps_, w_=w_, k=k, at_=at_, tcn=tcn, c=c: e.matmul(
                            ps_[:], lhsT=wsl(w_, k, 1024, c * 128, 128), rhs=at_[:, k * S + tcn * 512:k * S + tcn * 512 + 512],
                            start=(k == 0), stop=(k == 3)), reads=[wr__], writes=[rpsa if ps_ is psa else rpsb])
                if c == 0:
                    wg, wrg = wload(wsrc(win_d, 8, 3072, 512, rowlen=5120), 8, 512, ringC)
                    wg2, wrg2 = wload(wsrc(win_d, 8, 4096, 512, rowlen=5120), 8, 512, ringC)
                if c == 2:
                    wg_n = wload(wsrc(win_d, 8, 3072 + 512, 512, rowlen=5120), 8, 512, ringC)
                if c == 4:
                    wg, wrg = wg_n
                    wg2, wrg2 = wload(wsrc(win_d, 8, 4096 + 512, 512, rowlen=5120), 8, 512, ringC)
                if c == 6:
                    wo = [wload(wsrc(wout_d, 8, 0, 512, rowlen=D), 8, 512, ringC)]
                psg, rpsg = nb()
                psg2, rpsg2 = nb()
                for (ps_, w_, wr__, rr_) in ((psg, wg, wrg, rpsg), (psg2, wg2, wrg2, rpsg2)):
                    for k in range(8):
                        P.add("pe", lambda e, ps_=ps_, w_=w_, k=k, tcn=tcn, c=c: e.matmul(
                            ps_[:], lhsT=wsl(w_, k, 512, (c % 4) * 128, 128), rhs=hT[:, k * S + tcn * 512:k * S + tcn * 512 + 512],
                            start=(k == 0), stop=(k == 7)), reads=hT_all + [wr__], writes=[rr_])
                st_ = c % 2
                sa = RB[:, st_ * 1024: st_ * 1024 + 512]
                sb_ = RB[:, st_ * 1024 + 512: st_ * 1024 + 1024]
                ra_, rb_ = r_pc[st_]
                P.add("act", lambda e, psg=psg, sa=sa: e.activation(out=sa, in_=psg[:], func=AF.Sigmoid), reads=[rpsg], writes=[ra_])
                P.add("act", lambda e, psg2=psg2, sb_=sb_: e.activation(out=sb_, in_=psg2[:], func=AF.Sigmoid), reads=[rpsg2], writes=[rb_])
                P.add("dve", lambda e, psa=psa, sa=sa: e.tensor_tensor(out=sa, in0=psa[:], in1=sa, op=ALU.mult), reads=[rpsa], writes=[ra_])
                P.add("dve", lambda e, psb=psb, sb_=sb_: e.tensor_tensor(out=sb_, in0=psb[:], in1=sb_, op=ALU.mult), reads=[rpsb], writes=[rb_])
                if pend_add is not None:
                    pend_add()
                pend_add = (lambda c=c, sa=sa, sb_=sb_, ra_=ra_, rb_=rb_: P.add(
                    "dve", lambda e: e.tensor_tensor(out=gT[:, c * 512:(c + 1) * 512], in0=sa, in1=sb_, op=ALU.add),
                    reads=[ra_, rb_], writes=[r_mg]))
            pend_add()
            pend_add = None
            wo.append(wload(wsrc(wout_d, 8, 512, 512, rowlen=D), 8, 512, ringC))
            for tt in range(4):
                i = tcn * 4 + tt
                pss = []
                for hf in range(2):
                    ps, rps = nb()
                    pss.append((ps, rps))
                    for k in range(8):
                        P.add("pe", lambda e, ps=ps, k=k, tt=tt, hf=hf, wo=wo: e.matmul(
                            ps[:], lhsT=gT[:, k * 512 + tt * 128:k * 512 + tt * 128 + 128], rhs=wsl(wo[hf][0], k, 512, 0, 512),
                            start=(k == 0), stop=(k == 7)), reads=[r_mg, wo[hf][1]], writes=[rps])
                residual_epilogue(P, pss, x1tile(i), r_x, gt1bc, r_bc, stat, junk, mhalf, r_const, Res())
        P.barrier()

        dump('x1', RA[:])
        x1src = lambda tcn: [x1tile(tcn * 4 + tt) for tt in range(4)]
        dstn = [RB[:, tt * D:(tt + 1) * D] for tt in range(4)]
        ln_stats(x1src(0), r_x1all, 0)
        for tcn in range(4):
            if tcn + 1 < 4:
                ln_stats(x1src(tcn + 1), r_x1all, tcn + 1)
            ln_apply(x1src(tcn), r_x1all, tcn, vec[:, 24:32], vec[:, 32:40], dstn, r_xn2)
        P.barrier()

        dump('h2T', hT[:])
        r_hh = Res()
        P.add("dve", lambda e: e.tensor_copy(out=bass.AP(h2halo, 0, [[64, 128], [8, 8], [2, 3], [1, 2]]),
                                             in_=bass.AP(hT, 511, [[HT_L, 128], [S, 8], [512, 3], [1, 2]])), reads=hT_all, writes=[r_hh])
        r_E = [[Res(), Res()] for _ in range(22)]
        r_gTj = [Res() for _ in range(22)]
        r_mo = [Res() for _ in range(4)]
        for tcn in range(4):
            pend = None
            for j in range(22):
                g_, jj = j // 4, j % 4
                ncol = 512 if g_ < 5 else 256
                if jj == 0:
                    tv_ = wload(wsrc(wup_d, 8, g_ * 512, ncol, rowlen=5632), 8, ncol, ringF)
                    tg_ = wload(wsrc(wup_d, 8, 2816 + g_ * 512, ncol, rowlen=5632), 8, ncol, ringF)
                st_ = ffs[0] % 3
                ffs[0] += 1
                bufs = (attAf[:, st_ * 1024: st_ * 1024 + 512], attAf[:, st_ * 1024 + 512: st_ * 1024 + 1024])
                ph, rph = pb[6 + j % 2], rb[6 + j % 2]
                mains = []
                for isg in range(2):
                    wt, wr_ = (tv_, tg_)[isg]
                    ps, rps = pb[ffb[0] % 6], rb[ffb[0] % 6]
                    ffb[0] += 1
                    mains.append((ps, rps))
                    for k in range(8):
                        P.add("pe", lambda e, ps=ps, k=k, wt=wt, tcn=tcn, ncol=ncol, jj=jj: e.matmul(
                            ps[:], lhsT=wsl(wt, k, ncol, jj * 128, 128), rhs=hT[:, k * S + tcn * 512:k * S + tcn * 512 + 512],
                            start=(k == 0), stop=(k == 7)), reads=hT_all + [wr_], writes=[rps])
                    if tcn == 0:
                        for k in range(8):
                            P.add("pe", lambda e, ph=ph, k=k, wt=wt, ncol=ncol, jj=jj, isg=isg: e.matmul(
                                ph[:, 8 * isg:8 * isg + 6], lhsT=wsl(wt, k, ncol, jj * 128, 128), rhs=h2halo[:, k * 8:k * 8 + 6],
                                start=(k == 0), stop=(k == 7), skip_group_check=True), reads=[r_hh, wr_], writes=[rph])
                        P.add("act", lambda e, ph=ph, isg=isg, j=j: e.activation(
                            out=Etab[:, (isg * 22 + j) * 6:(isg * 22 + j) * 6 + 6], in_=ph[:, 8 * isg:8 * isg + 6], func=AF.Identity),
                            reads=[rph], writes=[r_E[j][isg]])
                cw = []
                for isg in range(2):
                    ch = isg * 22 + j
                    cw.append((spt[:, SP_CW0 + ch:SP_CW0 + ch + 1], spt[:, SP_CW1 + ch:SP_CW1 + ch + 1],
                               spt[:, SP_CW2 + ch:SP_CW2 + ch + 1], spt[:, SP_CB + ch:SP_CB + ch + 1]))
                for isg in range(2):
                    P.add("act", lambda e, t=bufs[isg], ps=mains[isg][0], w1=cw[isg][1], cb=cw[isg][3]: e.activation(
                        out=t, in_=ps[:], func=AF.Identity, scale=w1, bias=cb), reads=[mains[isg][1], r_const], writes=[r_ff[st_][isg]])
                for isg in range(2):
                    e0 = (isg * 22 + j) * 6
                    if tcn > 0:
                        P.add("act", lambda e, t=bufs[isg], e0=e0, w0=cw[isg][0], tcn=tcn: e.activation(
                            out=t[:, 0:1], in_=Etab[:, e0 + 2 * (tcn - 1):e0 + 2 * (tcn - 1) + 1], func=AF.Identity, scale=w0, bias=t[:, 0:1]),
                            reads=[r_E[j][isg], r_const], writes=[r_ff[st_][isg]])
                    if tcn < 3:
                        P.add("act", lambda e, t=bufs[isg], e0=e0, w2=cw[isg][2], tcn=tcn: e.activation(
                            out=t[:, 511:512], in_=Etab[:, e0 + 2 * tcn + 1:e0 + 2 * tcn + 2], func=AF.Identity, scale=w2, bias=t[:, 511:512]),
                            reads=[r_E[j][isg], r_const], writes=[r_ff[st_][isg]])
                if pend is not None:
                    pend[0]()
                for isg in range(2):
                    P.add("dve", lambda e, t=bufs[isg], ps=mains[isg][0], w0=cw[isg][0]: e.scalar_tensor_tensor(
                        out=t[:, 1:512], in0=ps[:, 0:511], scalar=w0, in1=t[:, 1:512], op0=ALU.mult, op1=ALU.add),
                        reads=[mains[isg][1], r_const], writes=[r_ff[st_][isg]])
                for isg in range(2):
                    P.add("dve", lambda e, t=bufs[isg], ps=mains[isg][0], w2=cw[isg][2]: e.scalar_tensor_tensor(
                        out=t[:, 0:511], in0=ps[:, 1:512], scalar=w2, in1=t[:, 0:511], op0=ALU.mult, op1=ALU.add),
                        reads=[mains[isg][1], r_const], writes=[r_ff[st_][isg]])
                if pend is not None:
                    pend[1]()
                gelu_fn = (lambda bufs=bufs, st_=st_: P.add("act", lambda e: e.activation(out=bufs[1], in_=bufs[1], func=AF.Gelu_apprx_tanh),
                                                           reads=[], writes=[r_ff[st_][1]]))
                mult_fn = (lambda bufs=bufs, st_=st_, j=j: P.add("dve", lambda e: e.tensor_tensor(
                    out=gT[:, j * 512:(j + 1) * 512], in0=bufs[1], in1=bufs[0], op=ALU.mult), reads=[r_ff[st_][0], r_ff[st_][1]], writes=[r_gTj[j]]))
                pend = (gelu_fn, mult_fn)
            pend[0]()
            pend[1]()
            if pend_epi[0] is not None:
                pend_epi[0](None)
                pend_epi[0] = None
            mo_res = r_mo
            for hf in range(2):
                pss = [nb() for _ in range(4)]
                for (k0, kc) in ((0, 8), (8, 8), (16, 6)):
                    wt, wr_ = wload(wsrc(wdn_d, 22, hf * 512, 512, k0=k0, kc=kc, rowlen=D), kc, 512, ringF)
                    for tt in range(4):
                        ps, rps = pss[tt]
                        for kk in range(kc):
                            k = k0 + kk
                            P.add("pe", lambda e, ps=ps, k=k, kk=kk, tt=tt, wt=wt: e.matmul(
                                ps[:], lhsT=gT[:, k * 512 + tt * 128:k * 512 + tt * 128 + 128], rhs=wsl(wt, kk, 512, 0, 512),
                                start=(k == 0), stop=(k == 21), skip_group_check=True), reads=[r_gTj[k], wr_], writes=[rps])
                for tt in range(4):
                    ps, rps = pss[tt]
                    P.add("act", lambda e, ps=ps, tt=tt, hf=hf: e.activation(out=RB[:, tt * D + hf * 512: tt * D + hf * 512 + 512], in_=ps[:], func=AF.Identity),
                          reads=[rps], writes=[mo_res[tt]])
            def epi(tt_sel, tcn=tcn, s=s, mo_res=mo_res):
                for tt in ([tt_sel] if tt_sel is not None else range(4)):
                    i = tcn * 4 + tt
                    r_y = Res()
                    residual_epilogue(P, [(RB[:, tt * D: tt * D + 512], mo_res[tt]), (RB[:, tt * D + 512: tt * D + 1024], mo_res[tt])],
                                      x1tile(i), Res(), gt2bc, r_bc, stat, junk, mhalf, r_const, r_y, sb=True)
                    dsty = bass.AP(y_d, (s * S + i * 128) * D, [[D, 128], [1, D]])
                    P.add("pool", lambda e, dsty=dsty, i=i: e.dma_start(out=dsty, in_=x1tile(i)), reads=[r_y], dma=True)
            pend_epi[0] = epi
        pend_epi[0](None)
        pend_epi[0] = None
        P.barrier()
    P.barrier()
    P.emit()
    return nc


_ctr = [0]
_rst = [None]


def residual_epilogue(P, halves, xt, r_x, gbc, r_bc, stat, junk, mhalf, r_const, r_out, sb=False):
    if _rst[0] is None:
        _rst[0] = [Res() for _ in range(5)]
    r_junk = _rst[0][4]
    ci = _ctr[0] % 4
    c = ci * 4
    _ctr[0] += 1
    r_st = _rst[0][ci]
    srcs = [(ps if sb else ps[:], rps) for (ps, rps) in halves]
    for hf, (src, rps) in enumerate(srcs):
        P.add("act", lambda e, src=src, hf=hf: e.activation(out=junk[:, 0:512], in_=src, func=AF.Square,
                                                            accum_out=stat[:, 32 + c + hf:33 + c + hf]), reads=[rps], writes=[r_st, r_junk])
    P.add("dve", lambda e: e.tensor_tensor(out=stat[:, 34 + c:35 + c], in0=stat[:, 32 + c:33 + c], in1=stat[:, 33 + c:34 + c], op=ALU.add),
          reads=[r_st], writes=[r_st])
    P.add("dve", lambda e: e.tensor_scalar(out=stat[:, 35 + c:36 + c], in0=stat[:, 34 + c:35 + c], scalar1=1.0 / D, scalar2=EPS,
                                           op0=ALU.mult, op1=ALU.add), reads=[r_st], writes=[r_st])
    P.add("pool", lambda e: e.tensor_tensor(out=stat[:, 48 + ci:49 + ci], in0=stat[:, 35 + c:36 + c], in1=mhalf[:, 0:1], op=ALU.pow),
          reads=[r_st, r_const], writes=[r_st])
    for hf, (src, rps) in enumerate(srcs):
        P.add("dve", lambda e, src=src, hf=hf: e.scalar_tensor_tensor(out=src, in0=src, scalar=stat[:, 48 + ci:49 + ci],
                                                                   in1=gbc[:, hf * 512:(hf + 1) * 512], op0=ALU.mult, op1=ALU.mult),
              reads=[r_st, r_bc], writes=[rps])
        P.add("dve", lambda e, src=src, hf=hf: e.tensor_tensor(out=xt[:, hf * 512:(hf + 1) * 512], in0=src, in1=xt[:, hf * 512:(hf + 1) * 512], op=ALU.add),
              reads=[rps, r_x], writes=[r_out])


_NC = [None]


def _prep(x_prompt, x_sample, c_prompt, c_sample, w_ada, b_ada, g_mix_pre, g_mix_post, g_ffn_pre, g_ffn_post,
          w_in, rpb, w_branch_a, w_branch_b, w_out, w_up, conv_w, conv_b, w_down):
    f = lambda a: np.ascontiguousarray(np.asarray(a, dtype=np.float32))
    xs = np.concatenate([f(x_prompt), f(x_sample)], axis=0)
    cs = np.concatenate([f(c_prompt), f(c_sample)], axis=0)
    COS, SIN, perm = _rope_tables()
    rr, cr, m1, m2 = _na_index()
    consts = np.zeros((128, NCC), np.float32)
    consts[:, C_ID:C_ID + 128] = np.eye(128, dtype=np.float32)
    consts[:, C_PERM:C_PERM + 128] = perm
    xx = np.arange(384)[None, :]
    pp = np.arange(128)[:, None]
    consts[:, C_T:C_T + 384] = np.where(np.abs(xx - pp - 64) <= 64, 0.0, NEG)
    consts[:, C_M1:C_M1 + 896] = np.where(m1, 1e30, NEG)
    consts[:, C_A1:C_A1 + 896] = np.where(m2, 1e30, NEG)
    consts[:, C_COS:C_COS + S] = COS
    consts[:, C_SIN:C_SIN + S] = SIN
    consts[:, C_ONES:C_ONES + 128] = 1.0
    G = np.ascontiguousarray(f(rpb)[0][:, rr, cr])
    spb = np.zeros((128, NSP), np.float32)
    spb[:, SP_GPRE:SP_GPRE + 8] = _fm(f(g_mix_pre)[0], 8)
    spb[:, SP_GPOST:SP_GPOST + 8] = _fm(f(g_mix_post)[0], 8)
    spb[:, SP_FPRE:SP_FPRE + 8] = _fm(f(g_ffn_pre)[0], 8)
    spb[:, SP_FPOST:SP_FPOST + 8] = _fm(f(g_ffn_post)[0], 8)
    spb[:, SP_BADA:SP_BADA + 48] = _fm(f(b_ada)[0], 48)
    cw = f(conv_w)[0]
    spb[:, SP_CW0:SP_CW0 + 44] = _fm(cw[0], 44)
    spb[:, SP_CW1:SP_CW1 + 44] = _fm(cw[1], 44)
    spb[:, SP_CW2:SP_CW2 + 44] = _fm(cw[2], 44)
    spb[:, SP_CB:SP_CB + 44] = _fm(f(conv_b)[0], 44)
    shared = {"consts": consts, "G": G, "w_ada": f(w_ada)[0], "w_in": f(w_in)[0], "w_ba": f(w_branch_a)[0],
              "w_bb": f(w_branch_b)[0], "w_out": f(w_out)[0], "w_up": f(w_up)[0], "w_down": f(w_down)[0]}
    in_maps = []
    for i in range(8):
        sp_i = spb.copy()
        ci = cs[3 * i:3 * i + 3]
        sp_i[:, SP_C:SP_C + 24] = ci.reshape(3, 8, 128).transpose(2, 1, 0).reshape(128, 24)
        m = dict(shared)
        m["x"] = np.ascontiguousarray(xs[3 * i:3 * i + 3])
        m["sp"] = sp_i
        in_maps.append(m)
    return in_maps


def kernel(**inputs):
    in_maps = _prep(**inputs)
    if _NC[0] is None:
        _NC[0] = build()
    nc = _NC[0]
    res = run_bass_kernel_spmd(nc, in_maps, core_ids=list(range(8)))
    ys = np.concatenate([np.asarray(r["y"], dtype=np.float32) for r in res.results], axis=0)
    return ys[:8].copy(), ys[8:].copy()
```

```python
import contextlib
import numpy as np
import concourse.bass as bass
import concourse.mybir as mybir
from concourse.bass_utils import run_bass_kernel_spmd

F32 = mybir.dt.float32
BF16 = mybir.dt.bfloat16
AF = mybir.ActivationFunctionType
ALU = mybir.AluOpType

ENGS = ("pe", "act", "dve", "pool", "sp")
NSLOT = 8
S = 2048
D = 1024
NSEQ = 3
NEG = -240000.0
EPS = 1e-6


class Res:
    __slots__ = ("w", "r")

    def __init__(self):
        self.w = None
        self.r = []


class Ins:
    __slots__ = ("eng", "fn", "deps", "sig", "cnt", "dma", "slot", "slotcnt", "qkey")

    def __init__(self, eng, fn, dma):
        self.eng = eng
        self.fn = fn
        self.deps = []
        self.sig = False
        self.cnt = 0
        self.dma = dma
        self.slot = 0
        self.slotcnt = 0
        self.qkey = eng


class Prog:
    def __init__(self, nc):
        self.nc = nc
        self.q = {e: [] for e in ENGS}
        self.ndma = {}
        self.lastdma = {}

    def add(self, eng, fn, reads=(), writes=(), dma=False, group=None):
        x = Ins(eng, fn, dma)
        deps = []
        for r in reads:
            if r.w is not None:
                deps.append(r.w)
        for r in writes:
            if r.w is not None:
                deps.append(r.w)
            deps.extend(r.r)
        for r in reads:
            if not dma:
                r.r = [y for y in r.r if not (y.eng == eng and not y.dma)]
            r.r.append(x)
        for r in writes:
            r.w = x
            r.r = []
        if dma:
            qk = eng if group is None else eng + ":" + group
            x.qkey = qk
            n = self.ndma.get(qk, 0)
            x.slot = n % NSLOT
            x.slotcnt = n // NSLOT + 1
            self.ndma[qk] = n + 1
            prev = self.lastdma.setdefault(qk, {}).get(x.slot)
            if prev is not None:
                deps.append(prev)
            self.lastdma[qk][x.slot] = x
        seen = set()
        for d in deps:
            if d is x or id(d) in seen:
                continue
            seen.add(id(d))
            if (not d.dma) and d.eng == "pe" and eng == "pe" and not dma:
                continue
            x.deps.append(d)
            if not d.dma:
                d.sig = True
        self.q[eng].append(x)
        return x

    def barrier(self):
        lasts = []
        for e in ENGS:
            comp = [y for y in self.q[e] if not y.dma and y.fn is not None]
            if comp:
                comp[-1].sig = True
                lasts.append(comp[-1])
            for _, y in self.lastdma.get(e, {}).items():
                lasts.append(y)
        for e in ENGS:
            x = Ins(e, None, False)
            x.deps = list(lasts)
            self.q[e].append(x)

    def emit(self):
        nc = self.nc
        for e in ENGS:
            c = 0
            for x in self.q[e]:
                if x.dma or x.fn is None:
                    continue
                if x.sig:
                    c += 1
                    x.cnt = c
        with contextlib.ExitStack() as st:
            csem = {e: st.enter_context(nc.semaphore("c_" + e)) for e in ENGS}
            dsem = {qk: [st.enter_context(nc.semaphore("d_%s%d" % (qk.replace(":", "_"), i))) for i in range(NSLOT)]
                    for qk in self.ndma}
            block = st.enter_context(nc.Block())
            engobj = {"pe": "tensor", "act": "scalar", "dve": "vector", "pool": "gpsimd", "sp": "sync"}

            def run(e, eng):
                waited = {}
                for x in self.q[e]:
                    for d in x.deps:
                        if d.dma:
                            key = (d.qkey, d.slot)
                            val = d.slotcnt
                            if waited.get(key, 0) >= val:
                                continue
                            waited[key] = val
                            eng.wait_ge(dsem[d.qkey][d.slot], 16 * val)
                        else:
                            key = d.eng
                            val = d.cnt
                            if waited.get(key, 0) >= val:
                                continue
                            waited[key] = val
                            eng.wait_ge(csem[d.eng], val)
                    if x.fn is None:
                        continue
                    bi = x.fn(eng)
                    if x.dma:
                        bi.then_inc(dsem[x.qkey][x.slot], 16)
                    elif x.sig:
                        bi.then_inc(csem[e], 1)

            for e in ENGS:
                if self.q[e]:
                    getattr(block, engobj[e])(lambda eng, e=e: run(e, eng))


def _rope_tables():
    half = 8
    inv = np.power(np.float64(500000.0), -np.arange(half, dtype=np.float64) / half)
    pos = np.arange(S, dtype=np.float64)
    ang = pos[:, None] * inv[None, :]
    c = np.cos(ang).astype(np.float32).T
    s = np.sin(ang).astype(np.float32).T
    COS = np.ones((128, S), np.float32)
    SIN = np.zeros((128, S), np.float32)
    for b in (0, 64):
        COS[b:b + 8] = c
        COS[b + 8:b + 16] = c
        SIN[b:b + 8] = -s
        SIN[b + 8:b + 16] = s
    perm = np.zeros((128, 128), np.float32)
    for b in (0, 64):
        for i in range(8):
            perm[b + 8 + i, b + i] = 1.0
            perm[b + i, b + 8 + i] = 1.0
    return COS, SIN, perm


def _na_index():
    Rk = np.arange(128) // 64
    ck = np.arange(128) % 64
    rho = np.arange(-6, 8)
    c = np.arange(64)
    row_rel = Rk[:, None, None] - rho[None, :, None] + 7 + 0 * c[None, None, :]
    col_rel = ck[:, None, None] - c[None, None, :] + 15 + 0 * rho[None, :, None]
    ok = (row_rel >= 0) & (row_rel <= 14) & (col_rel >= 0) & (col_rel <= 30)
    cs = np.clip(c - 8, 0, 48)
    colvalid = (ck[:, None, None] >= cs[None, None, :]) & (ck[:, None, None] < cs[None, None, :] + 16) \
        & (rho[None, :, None] > -100)
    d = Rk[:, None, None] - rho[None, :, None] + 0 * c[None, None, :]
    rowvalid = (d >= -4) & (d <= 3)
    m1 = (ok & colvalid & rowvalid).reshape(128, 896)
    m2 = (ok & colvalid).reshape(128, 896)
    return np.clip(row_rel, 0, 14).reshape(128, 896), np.clip(col_rel, 0, 30).reshape(128, 896), m1, m2


C_ID = 0
C_PERM = 128
C_T = 256
C_M1 = 640
C_A1 = 1536
C_M2 = 2432
C_A2 = 3328
C_COS = 4224
C_SIN = 6272
C_ONES = 8320
NCC = 8448

SP_GPRE, SP_GPOST, SP_FPRE, SP_FPOST, SP_BADA, SP_CW0, SP_CW1, SP_CW2, SP_CB, SP_C = 0, 8, 16, 24, 32, 80, 124, 168, 212, 256
NSP = 280


def _fm(v, nch):
    return np.ascontiguousarray(v.reshape(nch, 128).T)


def build(debug=False):
    nc = bass.Bass("TRN2", target_bir_lowering=False)
    nseq = 1 if debug else NSEQ
    dt = lambda name, shape, kind="ExternalInput": nc.dram_tensor(name, shape, F32, kind=kind)
    x_d = dt("x", [NSEQ, S, D])
    y_d = dt("y", [NSEQ, S, D], "ExternalOutput")
    consts_d = dt("consts", [128, NCC])
    sp_d = dt("sp", [128, NSP])
    G_d = dt("G", [8, 128, 896])
    wada_d = dt("w_ada", [D, 6 * D])
    win_d = dt("w_in", [D, 5120])
    wba_d = dt("w_ba", [512, D])
    wbb_d = dt("w_bb", [512, D])
    wout_d = dt("w_out", [D, D])
    wup_d = dt("w_up", [D, 5632])
    wdn_d = dt("w_down", [2816, D])

    P = Prog(nc)
    A = nc.alloc_sbuf_tensor
    if debug:
        dbg = {
            "hT": nc.dram_tensor("dbg_hT", [128, 8 * S], BF16, kind="ExternalOutput"),
            "h2T": nc.dram_tensor("dbg_h2T", [128, 8 * S], BF16, kind="ExternalOutput"),
            "attA": nc.dram_tensor("dbg_attA", [128, 4 * S], BF16, kind="ExternalOutput"),
            "attB": nc.dram_tensor("dbg_attB", [128, 4 * S], BF16, kind="ExternalOutput"),
            "x1": nc.dram_tensor("dbg_x1", [128, 16 * D], F32, kind="ExternalOutput"),
            "modT": nc.dram_tensor("dbg_modT", [128, 144], F32, kind="ExternalOutput"),
            "gt1bc": nc.dram_tensor("dbg_gt1bc", [128, D], F32, kind="ExternalOutput"),
            "QK": nc.dram_tensor("dbg_QK", [128, 4096], BF16, kind="ExternalOutput"),
            "V": nc.dram_tensor("dbg_V", [128, 12288], BF16, kind="ExternalOutput"),
        }

    def dump(name, src):
        if debug:
            P.barrier()
            P.add("sp", lambda e: e.dma_start(out=dbg[name].ap(), in_=src), dma=True)
            P.barrier()

    identf = A("identf", [128, 128], F32)
    onesf = A("onesf", [128, 128], F32)
    identb = A("identb", [128, 128], BF16)
    permb = A("permb", [128, 128], BF16)
    Tb = A("Tb", [128, 384], BF16)
    csb = A("csb", [128, 2 * S], BF16)
    cosb, sinb = csb[:, 0:S], csb[:, S:2 * S]
    spt = A("spt", [128, NSP], F32)
    modT = A("modT", [128, 144], F32)
    vec = A("vec", [128, 64], F32)
    gt1bc = A("gt1bc", [128, D], F32)
    gt2bc = A("gt2bc", [128, D], F32)
    stat = A("stat", [128, 64], F32)
    NW = 2
    wring = [A("wr%d" % i, [128, 8 * 512], BF16) for i in range(NW)]
    hT = A("hT", [128, 8 * S], BF16)
    attA = A("attA", [128, 4 * S], BF16)
    attB = A("attB", [128, 4 * S], BF16)
    RA = A("RA", [128, 16 * D], F32)
    RAb = RA.bitcast(BF16)
    VA_OFF = 0
    QT_OFF = 12288
    KT_OFF = 14336
    QRAW_OFF = 16384
    EXP_OFF = 17408
    QO_OFF = 30720
    ACC_OFF_F = 10240
    RDEN_OFF_F = 12288
    TMP_OFF_F = 14336
    RC = A("RC", [128, 9728], F32)
    RCb = RC.bitcast(BF16)
    RB = RC[:, 5632:9728]
    gT = RCb[:, 0:11264]
    badd = RCb[:, 0:14336]
    attAf = attA.bitcast(F32)
    t1v, t1g, sgs = attAf[:, 0:512], attAf[:, 512:1024], attAf[:, 1024:2048]
    t1v_c, t1g_c, sgs_c = RB[:, 0:512], RB[:, 512:1024], RB[:, 1024:2048]
    h2halo = A("h2halo", [128, 8 * 8], BF16)
    Etab = A("Etab", [128, 44 * 6], F32)
    junk = A("junk", [128, D], BF16)

    pb = [nc.alloc_psum_tensor("pb%d" % i, [128, 512], F32) for i in range(8)]
    rb = [Res() for _ in range(8)]
    bank_i = [0]
    ffb = [0]
    ffs = [0]
    pend_epi = [None]

    def nb():
        i = bank_i[0] % 8
        bank_i[0] += 1
        return pb[i], rb[i]

    def ap(t, rowlen, p0, np_, off, dims):
        return bass.AP(t, p0 * rowlen + off, [[rowlen, np_]] + [list(d) for d in dims])

    HT_L, ATT_L, RAB_L, RAF_L = 8 * S, 4 * S, 32768, 16 * D

    r_const = Res()
    r_t1v, r_t1g, r_sg0, r_sg1, r_halo, r_ta_p, r_tb_p, r_stln, r_badd = [Res() for _ in range(9)]
    r_qraw = [Res(), Res()]
    r_xb = [Res() for _ in range(4)]
    r_x1all, r_xn2, r_mgp = Res(), Res(), Res()
    r_Qp, r_Kp, r_Vp, r_accp, r_rdp = [Res() for _ in range(5)]
    r_pc = [(Res(), Res()), (Res(), Res())]
    r_ff = [(Res(), Res()), (Res(), Res()), (Res(), Res())]
    r_exb = [Res() for _ in range(4)]
    r_hT = [[Res() for _ in range(4)] for _ in range(8)]
    class Slot:
        def __init__(self, t, rowlen, off):
            self.t, self.rowlen, self.off, self.res = t, rowlen, off, Res()

    class Ring:
        def __init__(self, slots):
            self.slots, self.i = slots, 0

        def next(self):
            sl = self.slots[self.i % len(self.slots)]
            self.i += 1
            return sl

    sl_w0, sl_w1 = Slot(wring[0], 4096, 0), Slot(wring[1], 4096, 0)
    ring0 = Ring([sl_w0, sl_w1])
    sl_cs = Slot(csb, 2 * S, 0)
    r_cs = sl_cs.res
    ringC = Ring([sl_w0, sl_w1, sl_cs])
    ringAB = Ring([Slot(RCb, 19456, 4096), Slot(RCb, 19456, 15360)])
    ringF = Ring([sl_w0, sl_w1, Slot(attB, ATT_L, 0), Slot(attB, ATT_L, 4096)])

    def wload(srcd, kc, ncols, ring=None):
        src, deps = srcd
        sl = (ring or ring0).next()
        dst = bass.AP(sl.t, sl.off, [[sl.rowlen, 128], [ncols, kc], [1, ncols]])
        q_ = "sp" if deps else "pool"
        P.add(q_, lambda e, dst=dst, src=src: e.dma_start(out=dst, in_=src), reads=deps, writes=[sl.res], dma=True)
        return sl, sl.res

    def wsl(sl, k, ncols, c0, n):
        return bass.AP(sl.t, sl.off + k * ncols + c0, [[sl.rowlen, 128], [1, n]])

    wconv = {}

    cv_pending = []

    def convert(w_d, nrows, ncols, name):
        wb_d = nc.dram_tensor(name + "_bf", [nrows, ncols], BF16, kind="Internal")
        rs = []
        for c0 in range(0, ncols, 512):
            r = Res()
            cv_pending.append(lambda c0=c0, r=r, wb_d=wb_d, w_d=w_d: P.add(
                "pool", lambda e: e.dma_start(out=wb_d.ap()[:, c0:c0 + 512], in_=w_d.ap()[:, c0:c0 + 512]), writes=[r], dma=True, group="cv"))
            rs.append(r)
        wconv[id(w_d)] = (wb_d, rs)

    def convert_step(n):
        for _ in range(n):
            if cv_pending:
                cv_pending.pop(0)()

    def wsrc(w_d, nrows_k, c0, ncols, k0=0, kc=None, rowlen=None):
        kc = nrows_k if kc is None else kc
        deps = []
        if id(w_d) in wconv:
            w_d, rs = wconv[id(w_d)]
            deps = rs[c0 // 512:(c0 + ncols - 1) // 512 + 1]
        return bass.AP(w_d, (k0 * 128) * rowlen + c0, [[rowlen, 128], [128 * rowlen, kc], [1, ncols]]), deps

    P.add("sp", lambda e: e.dma_start(out=identf[:], in_=consts_d.ap()[:, C_ID:C_ID + 128]), writes=[r_const], dma=True)
    P.add("sp", lambda e: e.dma_start(out=onesf[:], in_=consts_d.ap()[:, C_ONES:C_ONES + 128]), writes=[r_const], dma=True)
    P.add("sp", lambda e: e.dma_start(out=spt[:], in_=sp_d.ap()), writes=[r_const], dma=True)
    for (t, c0, n) in ((identb, C_ID, 128), (permb, C_PERM, 128), (Tb, C_T, 384)):
        P.add("pool", lambda e, t=t, c0=c0, n=n: e.dma_start(out=t[:], in_=consts_d.ap()[:, c0:c0 + n]),
              writes=[r_const], dma=True)
    P.add("dve", lambda e: e.memset(RAb[:, 0:12288], 1.0), writes=[r_const])
    P.barrier()
    attBf = attB.bitcast(F32)
    r_gst = [Res() for _ in range(4)]
    r_dg_cur = [None]
    r_cap = Res()

    def gen_badd_part(part):
        if part == 0:
            P.add("sp", lambda e: e.dma_start(out=attAf[:, 0:1792], in_=consts_d.ap()[:, C_M1:C_M1 + 1792]), writes=[r_cap], dma=True)
        for h in (2 * part, 2 * part + 1):
            g = attBf[:, (h % 4) * 1024:(h % 4) * 1024 + 896]
            rg = r_gst[h % 4]
            P.add("sp", lambda e, g=g, h=h: e.dma_start(out=g, in_=G_d.ap()[h]), writes=[rg], dma=True)
            for v in range(2):
                o = badd[:, (h * 2 + v) * 896:(h * 2 + v + 1) * 896]
                P.add("dve", lambda e, o=o, g=g, v=v: e.scalar_tensor_tensor(out=o, in0=g, scalar=8.0, in1=attAf[:, v * 896:(v + 1) * 896],
                                                                          op0=ALU.mult, op1=ALU.min), reads=[rg, r_cap], writes=[r_badd, r_dg_cur[0]])

    silub = A("silub", [128, 24], BF16)
    r_sil = Res()
    P.add("act", lambda e: e.activation(out=silub[:], in_=spt[:, SP_C:SP_C + 24], func=AF.Silu), writes=[r_sil])
    modps, r_modps = pb[0], rb[0]
    for tile in range(12):
        wt, wr_ = wload(wsrc(wada_d, 8, tile * 512, 512, rowlen=6 * D), 8, 512)
        for sub in range(4):
            ch = tile * 4 + sub
            for k in range(8):
                P.add("pe", lambda e, wt=wt, sub=sub, k=k, ch=ch: e.matmul(
                    modps[:, ch * 3:ch * 3 + 3], lhsT=wsl(wt, k, 512, sub * 128, 128),
                    rhs=bass.AP(silub, k * 3, [[24, 128], [1, 3]]), start=(k == 0), stop=(k == 7), skip_group_check=True),
                    reads=[wr_, r_sil], writes=[r_modps])
    for (w_d_, nr_, ncl_, nm_) in ((win_d, D, 5120, "w_in"), (wba_d, 512, D, "w_ba"), (wbb_d, 512, D, "w_bb"), (wout_d, D, D, "w_out"),
                                   (wup_d, D, 5632, "w_up"), (wdn_d, 2816, D, "w_down")):
        convert(w_d_, nr_, ncl_, nm_)
    convert_step(10)
    r_mod = Res()
    P.add("dve", lambda e: e.tensor_tensor(
        out=bass.AP(modT, 0, [[144, 128], [3, 48], [1, 3]]), in0=bass.AP(modps, 0, [[512, 128], [3, 48], [1, 3]]),
        in1=bass.AP(spt, SP_BADA, [[NSP, 128], [1, 48], [0, 3]]), op=ALU.add), reads=[r_modps, r_const], writes=[r_mod])
    P.barrier()

    def modv(part, s):
        return bass.AP(modT, part * 24 + s, [[144, 128], [3, 8]])

    r_lnst = [(Res(), Res(), Res()), (Res(), Res(), Res())]
    _rst[0] = [Res() for _ in range(5)]
    r_junk_all = _rst[0][4]

    def ln_stats(src_tiles, src_res, tcn):
        r_ss, r_rs, r_rs2 = r_lnst[tcn % 2]
        so = 4 * (tcn % 2)
        for tt in range(4):
            P.add("act", lambda e, tt=tt: e.activation(out=junk[:], in_=src_tiles[tt], func=AF.Square,
                                                      accum_out=stat[:, so + tt:so + tt + 1]), reads=[src_res], writes=[r_ss, r_junk_all])
        P.add("dve", lambda e: e.tensor_scalar(out=stat[:, 8 + so:12 + so], in0=stat[:, so:so + 4], scalar1=1.0 / D, scalar2=EPS,
                                               op0=ALU.mult, op1=ALU.add), reads=[r_ss], writes=[r_rs])
        P.add("pool", lambda e: e.tensor_tensor(out=stat[:, 16 + so:20 + so], in0=stat[:, 8 + so:12 + so], in1=mhalf[:, 0:4], op=ALU.pow),
              reads=[r_rs, r_const], writes=[r_rs2])

    def ln_apply(src_tiles, src_res, tcn, sc_ap, sh_ap, inplace_dst, r_xn):
        r_ss, r_rs, r_rs2 = r_lnst[tcn % 2]
        so = 4 * (tcn % 2)
        for tt in range(4):
            P.add("dve", lambda e, tt=tt: e.tensor_scalar(out=inplace_dst[tt], in0=src_tiles[tt], scalar1=stat[:, 16 + so + tt:17 + so + tt],
                                                          scalar2=None, op0=ALU.mult), reads=[src_res, r_rs2], writes=[r_xn])
        for j in range(8):
            ps, rps = nb()
            for tt in range(4):
                P.add("pe", lambda e, ps=ps, tt=tt, j=j: e.transpose(out=ps[:, tt * 128:(tt + 1) * 128],
                                                                     in_=inplace_dst[tt][:, j * 128:(j + 1) * 128], identity=identf[:]),
                      reads=[r_xn, r_const], writes=[rps])
            o = hT[:, j * S + tcn * 512: j * S + tcn * 512 + 512]
            if j % 2 == 0:
                P.add("act", lambda e, o=o, ps=ps, j=j: e.activation(out=o, in_=ps[:], func=AF.Identity, scale=sc_ap[:, j:j + 1],
                                                                     bias=sh_ap[:, j:j + 1]), reads=[rps, r_vec], writes=[r_hT[j][tcn]])
            else:
                P.add("dve", lambda e, o=o, ps=ps, j=j: e.tensor_scalar(out=o, in0=ps[:], scalar1=sc_ap[:, j:j + 1], scalar2=sh_ap[:, j:j + 1],
                                                                        op0=ALU.mult, op1=ALU.add), reads=[rps, r_vec], writes=[r_hT[j][tcn]])

    epsb = A("epsb", [128, 1], F32)
    mhalf = A("mhalf", [128, 4], F32)
    P.add("dve", lambda e: e.memset(epsb[:], EPS), writes=[r_const])
    P.add("dve", lambda e: e.memset(mhalf[:], -0.5), writes=[r_const])
    r_vec = Res()
    hT_all = [r_hT[j][t] for j in range(8) for t in range(4)]

    def x1tile(i):
        return RA[:, i * D:(i + 1) * D]

    def xbuf(tcn):
        return [bass.AP(RA, tcn * 4096 + tt * 1024, [[RAF_L, 128], [1, 1024]]) for tt in range(4)]

    for s in range(nseq):
        for tcn in range(4):
            dstx = bass.AP(RA, tcn * 4096, [[RAF_L, 128], [1024, 4], [1, 1024]])
            srcx = bass.AP(x_d, (s * S + tcn * 512) * D, [[D, 128], [128 * D, 4], [1, D]])
            P.add("sp", lambda e, dstx=dstx, srcx=srcx: e.dma_start(out=dstx, in_=srcx), writes=[r_xb[tcn]], dma=True)
        P.add("dve", lambda e, s=s: e.scalar_tensor_tensor(out=vec[:, 0:8], in0=modv(1, s), scalar=1.0, in1=spt[:, SP_GPRE:SP_GPRE + 8],
                                                           op0=ALU.add, op1=ALU.mult), reads=[r_mod, r_const], writes=[r_vec])
        P.add("dve", lambda e, s=s: e.tensor_copy(out=vec[:, 8:16], in_=modv(0, s)), reads=[r_mod], writes=[r_vec])
        P.add("dve", lambda e, s=s: e.tensor_tensor(out=vec[:, 16:24], in0=modv(2, s), in1=spt[:, SP_GPOST:SP_GPOST + 8], op=ALU.mult),
              reads=[r_mod, r_const], writes=[r_vec])
        P.add("dve", lambda e, s=s: e.scalar_tensor_tensor(out=vec[:, 24:32], in0=modv(4, s), scalar=1.0, in1=spt[:, SP_FPRE:SP_FPRE + 8],
                                                           op0=ALU.add, op1=ALU.mult), reads=[r_mod, r_const], writes=[r_vec])
        P.add("dve", lambda e, s=s: e.tensor_copy(out=vec[:, 32:40], in_=modv(3, s)), reads=[r_mod], writes=[r_vec])
        P.add("dve", lambda e, s=s: e.tensor_tensor(out=vec[:, 40:48], in0=modv(5, s), in1=spt[:, SP_FPOST:SP_FPOST + 8], op=ALU.mult),
              reads=[r_mod, r_const], writes=[r_vec])
        r_bc = Res()
        r_dg = Res()
        r_dg_cur[0] = r_dg
        for (c0, dst) in ((16, gt1bc), (40, gt2bc)):
            dg = RB
            P.add("dve", lambda e, c0=c0: e.tensor_tensor(
                out=bass.AP(RC, 5632, [[9728, 128], [128, 8], [1, 128]]), in0=bass.AP(identf, 0, [[128, 128], [0, 8], [1, 128]]),
                in1=bass.AP(vec, c0, [[64, 128], [1, 8], [0, 128]]), op=ALU.mult), reads=[r_vec, r_const], writes=[r_dg])
            for hf in range(2):
                ps, rps = nb()
                P.add("pe", lambda e, ps=ps, hf=hf: e.matmul(ps[:], lhsT=onesf[:], rhs=RB[:, hf * 512:(hf + 1) * 512], start=True, stop=True),
                      reads=[r_dg, r_const], writes=[rps])
                P.add("act", lambda e, ps=ps, hf=hf, dst=dst: e.activation(out=dst[:, hf * 512:(hf + 1) * 512], in_=ps[:], func=AF.Identity),
                      reads=[rps], writes=[r_bc])

        ln_stats(xbuf(0), r_xb[0], 0)
        for tcn in range(4):
            if tcn + 1 < 4:
                ln_stats(xbuf(tcn + 1), r_xb[tcn + 1], tcn + 1)
            ln_apply(xbuf(tcn), r_xb[tcn], tcn, vec[:, 0:8], vec[:, 8:16], xbuf(tcn), r_xb[tcn])
            gen_badd_part(tcn)
        P.barrier()
        dump('hT', hT[:])
        dump('modT', modT[:])
        dump('gt1bc', gt1bc[:])
        P.add("dve", lambda e: e.memset(RAb[:, 0:12288], 1.0), writes=[r_const])
        P.add("dve", lambda e: e.memset(bass.AP(RAb, 64 * RAB_L + QT_OFF, [[RAB_L, 64], [1, S]]), 0.0), writes=[r_const])
        P.add("dve", lambda e: e.memset(bass.AP(RAb, QO_OFF, [[RAB_L, 64], [1, S]]), 0.0), writes=[r_const])
        P.barrier()

        P.add("pool", lambda e: e.dma_start(out=csb[:], in_=consts_d.ap()[:, C_COS:C_COS + 2 * S]), writes=[r_cs], dma=True)
        for mixer in range(2):
            if mixer == 1:
                dump('QK', RAb[:, QT_OFF:QT_OFF + 4096])
                dump('V', RAb[:, 0:12288])
            att = attA if mixer == 0 else attB
            qcol0 = 0 if mixer == 0 else 1536
            kcol0 = 512 if mixer == 0 else 2048
            vcol0 = 1024 if mixer == 0 else 2560
            norders = 3 if mixer == 0 else 1
            for hp in range(4):
                wt, wr_ = wload(wsrc(win_d, 8, vcol0 + hp * 128, 128, rowlen=5120), 8, 128)
                r_V = r_Vp
                for tcn in range(4):
                    ps, rps = nb()
                    for k in range(8):
                        P.add("pe", lambda e, ps=ps, k=k, tcn=tcn, wt=wt: e.matmul(
                            ps[:], lhsT=wsl(wt, k, 128, 0, 128), rhs=hT[:, k * S + tcn * 512:k * S + tcn * 512 + 512],
                            start=(k == 0), stop=(k == 7)), reads=hT_all + [wr_], writes=[rps])
                    P.add("act", lambda e, ps=ps, tcn=tcn: e.activation(out=RAb[:, EXP_OFF + tcn * 512:EXP_OFF + tcn * 512 + 512], in_=ps[:], func=AF.Identity),
                          reads=[rps], writes=[r_exb[tcn]])
                for o in range(norders):
                    for g4 in range(4):
                        ps, rps = nb()
                        psb16 = ps.bitcast(BF16)
                        for q in range(4):
                            i = g4 * 4 + q
                            if o == 0:
                                off, st_ = 128 * i, 1
                            elif o == 1:
                                off, st_ = 512 * (i % 4) + i // 4, 4
                            else:
                                off, st_ = i, 16
                            P.add("pe", lambda e, psb16=psb16, q=q, off=off, st_=st_: e.transpose(
                                out=psb16[:, q * 128:(q + 1) * 128], in_=bass.AP(RAb, EXP_OFF + off, [[RAB_L, 128], [st_, 128]]), identity=identb[:]),
                                reads=r_exb + [r_const], writes=[rps])
                        dst = bass.AP(RAb, VA_OFF + (o * 16 + g4 * 4) * 256, [[RAB_L, 128], [256, 4], [192, 2], [1, 64]])
                        srcp = bass.AP(psb16, 0, [[1024, 128], [128, 4], [64, 2], [1, 64]])
                        eng_ = "act" if g4 % 2 == 0 else "dve"
                        if eng_ == "act":
                            P.add("act", lambda e, dst=dst, srcp=srcp: e.activation(out=dst, in_=srcp, func=AF.Identity), reads=[rps], writes=[r_V])
                        else:
                            P.add("dve", lambda e, dst=dst, srcp=srcp: e.tensor_copy(out=dst, in_=srcp), reads=[rps], writes=[r_V])
                r_Q, r_K = r_Qp, r_Kp
                convert_step(3)
                for (col0, toff, rres) in ((qcol0, QT_OFF, r_Q), (kcol0, KT_OFF, r_K)):
                    wt, wr_ = wload(wsrc(win_d, 8, col0 + hp * 128, 128, rowlen=5120), 8, 128)
                    for tcn in range(4):
                        ps, rps = nb()
                        for k in range(8):
                            P.add("pe", lambda e, ps=ps, k=k, tcn=tcn, wt=wt: e.matmul(
                                ps[:], lhsT=wsl(wt, k, 128, 0, 128), rhs=hT[:, k * S + tcn * 512:k * S + tcn * 512 + 512],
                                start=(k == 0), stop=(k == 7)), reads=hT_all + [wr_], writes=[rps])
                        dstq = RAb[:, toff + tcn * 512: toff + tcn * 512 + 512]
                        isq = (toff == QT_OFF)
                        dq_e = bass.AP(RAb, QT_OFF + tcn * 512, [[RAB_L, 64], [1, 512]])
                        dq_o = bass.AP(RAb, 64 * RAB_L + QO_OFF + tcn * 512, [[RAB_L, 64], [1, 512]])
                        if mixer == 1 and isq:
                            P.add("act", lambda e, dq_e=dq_e, ps=ps: e.activation(out=dq_e, in_=ps[0:64, :], func=AF.Identity),
                                  reads=[rps], writes=[rres])
                            P.add("act", lambda e, dq_o=dq_o, ps=ps: e.activation(out=dq_o, in_=ps[64:128, :], func=AF.Identity),
                                  reads=[rps], writes=[rres])
                        elif mixer == 1:
                            P.add("act", lambda e, dstq=dstq, ps=ps: e.activation(out=dstq, in_=ps[:], func=AF.Identity),
                                  reads=[rps], writes=[rres])
                        else:
                            qraw = RAb[:, QRAW_OFF + (tcn % 2) * 512: QRAW_OFF + (tcn % 2) * 512 + 512]
                            r_qr = r_qraw[tcn % 2]
                            P.add("act", lambda e, qraw=qraw, ps=ps: e.activation(out=qraw, in_=ps[:], func=AF.Identity),
                                  reads=[rps], writes=[r_qr])
                            ps2, rps2 = nb()
                            P.add("pe", lambda e, ps2=ps2, qraw=qraw: e.matmul(ps2[:], lhsT=permb[:], rhs=qraw, start=True, stop=True),
                                  reads=[r_qr, r_const], writes=[rps2])
                            ta = RA[:, TMP_OFF_F: TMP_OFF_F + 512]
                            tb_ = RA[:, TMP_OFF_F + 512: TMP_OFF_F + 1024]
                            r_ta, r_tb = r_ta_p, r_tb_p
                            P.add("dve", lambda e, ta=ta, ps2=ps2, tcn=tcn: e.tensor_tensor(out=ta, in0=ps2[:], in1=sinb[:, tcn * 512:(tcn + 1) * 512],
                                                                                          op=ALU.mult), reads=[rps2, r_cs], writes=[r_ta])
                            P.add("dve", lambda e, tb_=tb_, qraw=qraw, tcn=tcn: e.tensor_tensor(out=tb_, in0=qraw, in1=cosb[:, tcn * 512:(tcn + 1) * 512],
                                                                                              op=ALU.mult), reads=[r_qr, r_cs], writes=[r_tb])
                            if isq:
                                P.add("dve", lambda e, dq_e=dq_e, ta=ta, tb_=tb_: e.tensor_tensor(out=dq_e, in0=ta[0:64, :], in1=tb_[0:64, :], op=ALU.add),
                                      reads=[r_ta, r_tb], writes=[rres])
                                P.add("dve", lambda e, dq_o=dq_o, ta=ta, tb_=tb_: e.tensor_tensor(out=dq_o, in0=ta[64:128, :], in1=tb_[64:128, :], op=ALU.add),
                                      reads=[r_ta, r_tb], writes=[rres])
                            else:
                                P.add("dve", lambda e, dstq=dstq, ta=ta, tb_=tb_: e.tensor_tensor(out=dstq, in0=ta, in1=tb_, op=ALU.add),
                                      reads=[r_ta, r_tb], writes=[rres])
                for hl in range(2):
                    h = hp * 2 + hl
                    b = 64 * hl
                    accb = [pb[i] for i in range(4)]
                    accr = [rb[i] for i in range(4)]
                    scb = [(pb[4 + i], rb[4 + i]) for i in range(4)]
                    sci = [0]
                    r_acc = r_accp
                    acc_sb = lambda c0, st_, n, p0=0, np_=128: bass.AP(RA, p0 * RAF_L + ACC_OFF_F + c0, [[RAF_L, np_], [st_, n]])

                    LOOK = 2
                    pipe = []

                    def qk_part(units):
                        slot = sci[0] % 4
                        sci[0] += 1
                        ps, rps = scb[slot]
                        col = 0
                        for (koff, kst, vtile, segs) in units:
                            lk = bass.AP(RAb, KT_OFF + koff, [[RAB_L, 128], [kst, 128]])
                            for (qoff, qst, nq, mrhs, bk, ac0) in segs:
                                rq = bass.AP(RAb, (QT_OFF if hl == 0 else QO_OFF) + qoff, [[RAB_L, 128], [qst, nq]])
                                P.add("pe", lambda e, ps=ps, col=col, nq=nq, lk=lk, rq=rq: e.matmul(
                                    ps[:, col:col + nq], lhsT=lk, rhs=rq, start=True, stop=False, skip_group_check=True),
                                    reads=[r_Q, r_K], writes=[rps])
                                P.add("pe", lambda e, ps=ps, col=col, nq=nq, mrhs=mrhs: e.matmul(
                                    ps[:, col:col + nq], lhsT=identb[:], rhs=mrhs, start=False, stop=True, skip_group_check=True),
                                    reads=[r_const, r_badd], writes=[rps])
                                col += nq
                        ex = RAb[:, EXP_OFF + slot * 512: EXP_OFF + slot * 512 + col]
                        P.add("act", lambda e, ex=ex, ps=ps, col=col: e.activation(out=ex, in_=ps[:, 0:col], func=AF.Exp, scale=0.125),
                              reads=[rps], writes=[r_exb[slot]])
                        return (slot, units)

                    def pv_part(ctx, touched):
                        slot, units = ctx
                        col = 0
                        for (koff, kst, vtile, segs) in units:
                            lv = bass.AP(RAb, VA_OFF + vtile * 256 + hl * 128, [[RAB_L, 128], [1, 128]])
                            for (qoff, qst, nq, mrhs, bk, ac0) in segs:
                                first = bk not in touched
                                touched.add(bk)
                                exs = bass.AP(RAb, EXP_OFF + slot * 512 + col, [[RAB_L, 128], [1, nq]])
                                P.add("pe", lambda e, bk=bk, ac0=ac0, nq=nq, lv=lv, exs=exs, first=first: e.matmul(
                                    accb[bk][:, ac0:ac0 + nq], lhsT=lv, rhs=exs, start=first, stop=True, skip_group_check=True),
                                    reads=[r_exb[slot], r_V], writes=[accr[bk]])
                                col += nq

                    def push(units, touched, after=None):
                        pipe.append((qk_part(units), touched, after))
                        if len(pipe) > LOOK:
                            c_, t_, a_ = pipe.pop(0)
                            pv_part(c_, t_)
                            if a_ is not None:
                                a_()

                    def flush():
                        while pipe:
                            c_, t_, a_ = pipe.pop(0)
                            pv_part(c_, t_)
                            if a_ is not None:
                                a_()

                    den_p0 = 64 - b
                    if mixer == 0:
                        def merge(o):
                            for bk in range(4):
                                if o == 0:
                                    dsta = acc_sb(bk * 512, 1, 512)
                                    P.add("act", lambda e, dsta=dsta, bk=bk: e.activation(out=dsta, in_=accb[bk][:], func=AF.Identity),
                                          reads=[accr[bk]], writes=[r_acc])
                                elif o == 1:
                                    dsta = acc_sb(bk, 4, 512)
                                    P.add("dve", lambda e, dsta=dsta, bk=bk: e.tensor_tensor(out=dsta, in0=accb[bk][:], in1=dsta, op=ALU.add),
                                          reads=[accr[bk], r_acc], writes=[r_acc])
                                else:
                                    dsta = bass.AP(RA, ACC_OFF_F + 4 * bk, [[RAF_L, 128], [1, 4], [16, 128]])
                                    srca = bass.AP(accb[bk], 0, [[512, 128], [128, 4], [1, 128]])
                                    P.add("dve", lambda e, dsta=dsta, srca=srca: e.tensor_tensor(out=dsta, in0=srca, in1=dsta, op=ALU.add),
                                          reads=[accr[bk], r_acc], writes=[r_acc])
                            if o == 2:
                                rden = bass.AP(RA, b * RAF_L + RDEN_OFF_F, [[RAF_L, 64], [1, S]])
                                r_rd = r_rdp
                                P.add("act", lambda e, rden=rden, den_p0=den_p0: e.activation(out=rden, in_=acc_sb(0, 1, S, den_p0, 64), func=AF.Ln),
                                      reads=[r_acc], writes=[r_rd])
                                P.add("act", lambda e, rden=rden: e.activation(out=rden, in_=rden, func=AF.Exp, scale=-1.0), writes=[r_rd])
                                dsto = bass.AP(att, b * ATT_L + hp * S, [[ATT_L, 64], [1, S]])
                                P.add("dve", lambda e, dsto=dsto, rden=rden, b=b: e.tensor_tensor(out=dsto, in0=acc_sb(0, 1, S, b, 64), in1=rden, op=ALU.mult),
                                      reads=[r_acc, r_rd], writes=[Res()])

                        for o, d in enumerate((1, 4, 16)):
                            L = S // d
                            touched = set()
                            ulist = []
                            for r in range(d):
                                for blk in range(L // 128):
                                    j0 = 128 * blk
                                    jq0, jq1 = max(0, j0 - 64), min(L, j0 + 192)
                                    segs = []
                                    pos = r * L + jq0
                                    end = r * L + jq1
                                    while pos < end:
                                        e_ = min(end, (pos // 512 + 1) * 512)
                                        jj = pos - r * L
                                        x0 = jj - j0 + 64
                                        segs.append((d * jj + r, d, e_ - pos, Tb[:, x0:x0 + (e_ - pos)], pos // 512, pos % 512))
                                        pos = e_
                                    ulist.append((d * j0 + r, d, o * 16 + (r * L + j0) // 128, segs))
                            groups = [ulist[i:i + 2] for i in range(0, len(ulist), 2)]
                            for gi, units in enumerate(groups):
                                push(units, touched, (lambda o=o: merge(o)) if gi == len(groups) - 1 else None)
                    else:
                        def norm_b():
                            for bk in range(4):
                                rden = bass.AP(RA, b * RAF_L + RDEN_OFF_F + bk * 512, [[RAF_L, 64], [1, 512]])
                                r_rd = r_rdp
                                P.add("act", lambda e, rden=rden, bk=bk, den_p0=den_p0: e.activation(out=rden, in_=accb[bk][den_p0:den_p0 + 64, :], func=AF.Ln),
                                      reads=[accr[bk]], writes=[r_rd])
                                P.add("act", lambda e, rden=rden: e.activation(out=rden, in_=rden, func=AF.Exp, scale=-1.0), writes=[r_rd])
                                dsto = bass.AP(att, b * ATT_L + hp * S + bk * 512, [[ATT_L, 64], [1, 512]])
                                P.add("dve", lambda e, dsto=dsto, rden=rden, bk=bk, b=b: e.tensor_tensor(out=dsto, in0=accb[bk][b:b + 64, :], in1=rden, op=ALU.mult),
                                      reads=[accr[bk], r_rd], writes=[Res()])

                        touched = set()
                        ulist = []
                        for m in range(16):
                            rp_lo, rp_hi = max(4, 2 * m - 3), min(28, 2 * m + 5)
                            r_lo = 0 if rp_lo == 4 else rp_lo
                            r_hi = 31 if rp_hi == 28 else rp_hi
                            r0 = r_lo
                            while r0 <= r_hi:
                                r1 = min(r_hi, (r0 // 8) * 8 + 7)
                                segs = []
                                rr = r0
                                while rr <= r1:
                                    var = 0 if 4 <= rr <= 28 else 1
                                    re_ = rr
                                    while re_ + 1 <= r1 and (0 if 4 <= re_ + 1 <= 28 else 1) == var:
                                        re_ += 1
                                    nq = 64 * (re_ - rr + 1)
                                    rho0 = rr - 2 * m
                                    tb0 = (h * 2 + var) * 896 + (rho0 + 6) * 64
                                    segs.append((64 * rr, 1, nq, badd[:, tb0:tb0 + nq], rr // 8, (64 * rr) % 512))
                                    rr = re_ + 1
                                ulist.append([(128 * m, 1, m, segs)])
                                r0 = r1 + 1
                        for gi, units in enumerate(ulist):
                            push(units, touched, norm_b if gi == len(ulist) - 1 else None)
                    flush()

        convert_step(100)
        P.barrier()
        dump('attA', attA[:])
        dump('attB', attB[:])
        for tcn in range(4):
            r_x = Res()
            dstx = bass.AP(RA, tcn * 4096, [[RAF_L, 128], [1024, 4], [1, 1024]])
            srcx = bass.AP(x_d, (s * S + tcn * 512) * D, [[D, 128], [128 * D, 4], [1, D]])
            P.add("sp", lambda e, dstx=dstx, srcx=srcx: e.dma_start(out=dstx, in_=srcx), writes=[r_x], dma=True)
            r_mg = r_mgp
            pend_add = None
            if tcn == 0:
                wa, wra = wload(wsrc(wba_d, 4, 0, 1024, rowlen=D), 4, 1024, ringAB)
                wb_, wrb = wload(wsrc(wbb_d, 4, 0, 1024, rowlen=D), 4, 1024, ringAB)
            for c in range(8):
                psa, rpsa = nb()
                psb, rpsb = nb()
                for (ps_, w_, wr__, at_) in ((psa, wa, wra, attA), (psb, wb_, wrb, attB)):
                    for k in range(4):
                        P.add("pe", lambda e, ps_=ps_, w_=w_, k=k, at_=at_, tcn=tcn, c=c: e.matmul(
                            ps_[:], lhsT=wsl(w_, k, 1024, c * 128, 128), rhs=at_[:, k * S + tcn * 512:k * S + tcn * 512 + 512],
                            start=(k == 0), stop=(k == 3)), reads=[wr__], writes=[rpsa if ps_ is psa else rpsb])
                if c == 0:
                    wg, wrg = wload(wsrc(win_d, 8, 3072, 512, rowlen=5120), 8, 512, ringC)
                    wg2, wrg2 = wload(wsrc(win_d, 8, 4096, 512, rowlen=5120), 8, 512, ringC)
                if c == 2:
                    wg_n = wload(wsrc(win_d, 8, 3072 + 512, 512, rowlen=5120), 8, 512, ringC)
                if c == 4:
                    wg, wrg = wg_n
                    wg2, wrg2 = wload(wsrc(win_d, 8, 4096 + 512, 512, rowlen=5120), 8, 512, ringC)
                if c == 6:
                    wo = [wload(wsrc(wout_d, 8, 0, 512, rowlen=D), 8, 512, ringC)]
                psg, rpsg = nb()
                psg2, rpsg2 = nb()
                for (ps_, w_, wr__, rr_) in ((psg, wg, wrg, rpsg), (psg2, wg2, wrg2, rpsg2)):
                    for k in range(8):
                        P.add("pe", lambda e, ps_=ps_, w_=w_, k=k, tcn=tcn, c=c: e.matmul(
                            ps_[:], lhsT=wsl(w_, k, 512, (c % 4) * 128, 128), rhs=hT[:, k * S + tcn * 512:k * S + tcn * 512 + 512],
                            start=(k == 0), stop=(k == 7)), reads=hT_all + [wr__], writes=[rr_])
                st_ = c % 2
                sa = RB[:, st_ * 1024: st_ * 1024 + 512]
                sb_ = RB[:, st_ * 1024 + 512: st_ * 1024 + 1024]
                ra_, rb_ = r_pc[st_]
                P.add("act", lambda e, psg=psg, sa=sa: e.activation(out=sa, in_=psg[:], func=AF.Sigmoid), reads=[rpsg], writes=[ra_])
                P.add("act", lambda e, psg2=psg2, sb_=sb_: e.activation(out=sb_, in_=psg2[:], func=AF.Sigmoid), reads=[rpsg2], writes=[rb_])
                P.add("dve", lambda e, psa=psa, sa=sa: e.tensor_tensor(out=sa, in0=psa[:], in1=sa, op=ALU.mult), reads=[rpsa], writes=[ra_])
                P.add("dve", lambda e, psb=psb, sb_=sb_: e.tensor_tensor(out=sb_, in0=psb[:], in1=sb_, op=ALU.mult), reads=[rpsb], writes=[rb_])
                if pend_add is not None:
                    pend_add()
                pend_add = (lambda c=c, sa=sa, sb_=sb_, ra_=ra_, rb_=rb_: P.add(
                    "dve", lambda e: e.tensor_tensor(out=gT[:, c * 512:(c + 1) * 512], in0=sa, in1=sb_, op=ALU.add),
                    reads=[ra_, rb_], writes=[r_mg]))
            pend_add()
            pend_add = None
            wo.append(wload(wsrc(wout_d, 8, 512, 512, rowlen=D), 8, 512, ringC))
            for tt in range(4):
                i = tcn * 4 + tt
                pss = []
                for hf in range(2):
                    ps, rps = nb()
                    pss.append((ps, rps))
                    for k in range(8):
                        P.add("pe", lambda e, ps=ps, k=k, tt=tt, hf=hf, wo=wo: e.matmul(
                            ps[:], lhsT=gT[:, k * 512 + tt * 128:k * 512 + tt * 128 + 128], rhs=wsl(wo[hf][0], k, 512, 0, 512),
                            start=(k == 0), stop=(k == 7)), reads=[r_mg, wo[hf][1]], writes=[rps])
                residual_epilogue(P, pss, x1tile(i), r_x, gt1bc, r_bc, stat, junk, mhalf, r_const, Res())
        P.barrier()

        dump('x1', RA[:])
        x1src = lambda tcn: [x1tile(tcn * 4 + tt) for tt in range(4)]
        dstn = [RB[:, tt * D:(tt + 1) * D] for tt in range(4)]
        ln_stats(x1src(0), r_x1all, 0)
        for tcn in range(4):
            if tcn + 1 < 4:
                ln_stats(x1src(tcn + 1), r_x1all, tcn + 1)
            ln_apply(x1src(tcn), r_x1all, tcn, vec[:, 24:32], vec[:, 32:40], dstn, r_xn2)
        P.barrier()

        dump('h2T', hT[:])
        r_hh = Res()
        P.add("dve", lambda e: e.tensor_copy(out=bass.AP(h2halo, 0, [[64, 128], [8, 8], [2, 3], [1, 2]]),
                                             in_=bass.AP(hT, 511, [[HT_L, 128], [S, 8], [512, 3], [1, 2]])), reads=hT_all, writes=[r_hh])
        r_E = [[Res(), Res()] for _ in range(22)]
        r_gTj = [Res() for _ in range(22)]
        r_mo = [Res() for _ in range(4)]
        for tcn in range(4):
            pend = None
            for j in range(22):
                g_, jj = j // 4, j % 4
                ncol = 512 if g_ < 5 else 256
                if jj == 0:
                    tv_ = wload(wsrc(wup_d, 8, g_ * 512, ncol, rowlen=5632), 8, ncol, ringF)
                    tg_ = wload(wsrc(wup_d, 8, 2816 + g_ * 512, ncol, rowlen=5632), 8, ncol, ringF)
                st_ = ffs[0] % 3
                ffs[0] += 1
                bufs = (attAf[:, st_ * 1024: st_ * 1024 + 512], attAf[:, st_ * 1024 + 512: st_ * 1024 + 1024])
                ph, rph = pb[6 + j % 2], rb[6 + j % 2]
                mains = []
                for isg in range(2):
                    wt, wr_ = (tv_, tg_)[isg]
                    ps, rps = pb[ffb[0] % 6], rb[ffb[0] % 6]
                    ffb[0] += 1
                    mains.append((ps, rps))
                    for k in range(8):
                        P.add("pe", lambda e, ps=ps, k=k, wt=wt, tcn=tcn, ncol=ncol, jj=jj: e.matmul(
                            ps[:], lhsT=wsl(wt, k, ncol, jj * 128, 128), rhs=hT[:, k * S + tcn * 512:k * S + tcn * 512 + 512],
                            start=(k == 0), stop=(k == 7)), reads=hT_all + [wr_], writes=[rps])
                    if tcn == 0:
                        for k in range(8):
                            P.add("pe", lambda e, ph=ph, k=k, wt=wt, ncol=ncol, jj=jj, isg=isg: e.matmul(
                                ph[:, 8 * isg:8 * isg + 6], lhsT=wsl(wt, k, ncol, jj * 128, 128), rhs=h2halo[:, k * 8:k * 8 + 6],
                                start=(k == 0), stop=(k == 7), skip_group_check=True), reads=[r_hh, wr_], writes=[rph])
                        P.add("act", lambda e, ph=ph, isg=isg, j=j: e.activation(
                            out=Etab[:, (isg * 22 + j) * 6:(isg * 22 + j) * 6 + 6], in_=ph[:, 8 * isg:8 * isg + 6], func=AF.Identity),
                            reads=[rph], writes=[r_E[j][isg]])
                cw = []
                for isg in range(2):
                    ch = isg * 22 + j
                    cw.append((spt[:, SP_CW0 + ch:SP_CW0 + ch + 1], spt[:, SP_CW1 + ch:SP_CW1 + ch + 1],
                               spt[:, SP_CW2 + ch:SP_CW2 + ch + 1], spt[:, SP_CB + ch:SP_CB + ch + 1]))
                for isg in range(2):
                    P.add("act", lambda e, t=bufs[isg], ps=mains[isg][0], w1=cw[isg][1], cb=cw[isg][3]: e.activation(
                        out=t, in_=ps[:], func=AF.Identity, scale=w1, bias=cb), reads=[mains[isg][1], r_const], writes=[r_ff[st_][isg]])
                for isg in range(2):
                    e0 = (isg * 22 + j) * 6
                    if tcn > 0:
                        P.add("act", lambda e, t=bufs[isg], e0=e0, w0=cw[isg][0], tcn=tcn: e.activation(
                            out=t[:, 0:1], in_=Etab[:, e0 + 2 * (tcn - 1):e0 + 2 * (tcn - 1) + 1], func=AF.Identity, scale=w0, bias=t[:, 0:1]),
                            reads=[r_E[j][isg], r_const], writes=[r_ff[st_][isg]])
                    if tcn < 3:
                        P.add("act", lambda e, t=bufs[isg], e0=e0, w2=cw[isg][2], tcn=tcn: e.activation(
                            out=t[:, 511:512], in_=Etab[:, e0 + 2 * tcn + 1:e0 + 2 * tcn + 2], func=AF.Identity, scale=w2, bias=t[:, 511:512]),
                            reads=[r_E[j][isg], r_const], writes=[r_ff[st_][isg]])
                if pend is not None:
                    pend[0]()
                for isg in range(2):
                    P.add("dve", lambda e, t=bufs[isg], ps=mains[isg][0], w0=cw[isg][0]: e.scalar_tensor_tensor(
                        out=t[:, 1:512], in0=ps[:, 0:511], scalar=w0, in1=t[:, 1:512], op0=ALU.mult, op1=ALU.add),
                        reads=[mains[isg][1], r_const], writes=[r_ff[st_][isg]])
                for isg in range(2):
                    P.add("dve", lambda e, t=bufs[isg], ps=mains[isg][0], w2=cw[isg][2]: e.scalar_tensor_tensor(
                        out=t[:, 0:511], in0=ps[:, 1:512], scalar=w2, in1=t[:, 0:511], op0=ALU.mult, op1=ALU.add),
                        reads=[mains[isg][1], r_const], writes=[r_ff[st_][isg]])
                if pend is not None:
                    pend[1]()
                gelu_fn = (lambda bufs=bufs, st_=st_: P.add("act", lambda e: e.activation(out=bufs[1], in_=bufs[1], func=AF.Gelu_apprx_tanh),
                                                           reads=[], writes=[r_ff[st_][1]]))
                mult_fn = (lambda bufs=bufs, st_=st_, j=j: P.add("dve", lambda e: e.tensor_tensor(
                    out=gT[:, j * 512:(j + 1) * 512], in0=bufs[1], in1=bufs[0], op=ALU.mult), reads=[r_ff[st_][0], r_ff[st_][1]], writes=[r_gTj[j]]))
                pend = (gelu_fn, mult_fn)
            pend[0]()
            pend[1]()
            if pend_epi[0] is not None:
                pend_epi[0](None)
                pend_epi[0] = None
            mo_res = r_mo
            for hf in range(2):
                pss = [nb() for _ in range(4)]
                for (k0, kc) in ((0, 8), (8, 8), (16, 6)):
                    wt, wr_ = wload(wsrc(wdn_d, 22, hf * 512, 512, k0=k0, kc=kc, rowlen=D), kc, 512, ringF)
                    for tt in range(4):
                        ps, rps = pss[tt]
                        for kk in range(kc):
                            k = k0 + kk
                            P.add("pe", lambda e, ps=ps, k=k, kk=kk, tt=tt, wt=wt: e.matmul(
                                ps[:], lhsT=gT[:, k * 512 + tt * 128:k * 512 + tt * 128 + 128], rhs=wsl(wt, kk, 512, 0, 512),
                                start=(k == 0), stop=(k == 21), skip_group_check=True), reads=[r_gTj[k], wr_], writes=[rps])
                for tt in range(4):
                    ps, rps = pss[tt]
                    P.add("act", lambda e, ps=ps, tt=tt, hf=hf: e.activation(out=RB[:, tt * D + hf * 512: tt * D + hf * 512 + 512], in_=ps[:], func=AF.Identity),
                          reads=[rps], writes=[mo_res[tt]])
            def epi(tt_sel, tcn=tcn, s=s, mo_res=mo_res):
                for tt in ([tt_sel] if tt_sel is not None else range(4)):
                    i = tcn * 4 + tt
                    r_y = Res()
                    residual_epilogue(P, [(RB[:, tt * D: tt * D + 512], mo_res[tt]), (RB[:, tt * D + 512: tt * D + 1024], mo_res[tt])],
                                      x1tile(i), Res(), gt2bc, r_bc, stat, junk, mhalf, r_const, r_y, sb=True)
                    dsty = bass.AP(y_d, (s * S + i * 128) * D, [[D, 128], [1, D]])
                    P.add("pool", lambda e, dsty=dsty, i=i: e.dma_start(out=dsty, in_=x1tile(i)), reads=[r_y], dma=True)
            pend_epi[0] = epi
        pend_epi[0](None)
        pend_epi[0] = None
        P.barrier()
    P.barrier()
    P.emit()
    return nc


_ctr = [0]
_rst = [None]


def residual_epilogue(P, halves, xt, r_x, gbc, r_bc, stat, junk, mhalf, r_const, r_out, sb=False):
    if _rst[0] is None:
        _rst[0] = [Res() for _ in range(5)]
    r_junk = _rst[0][4]
    ci = _ctr[0] % 4
    c = ci * 4
    _ctr[0] += 1
    r_st = _rst[0][ci]
    srcs = [(ps if sb else ps[:], rps) for (ps, rps) in halves]
    for hf, (src, rps) in enumerate(srcs):
        P.add("act", lambda e, src=src, hf=hf: e.activation(out=junk[:, 0:512], in_=src, func=AF.Square,
                                                            accum_out=stat[:, 32 + c + hf:33 + c + hf]), reads=[rps], writes=[r_st, r_junk])
    P.add("dve", lambda e: e.tensor_tensor(out=stat[:, 34 + c:35 + c], in0=stat[:, 32 + c:33 + c], in1=stat[:, 33 + c:34 + c], op=ALU.add),
          reads=[r_st], writes=[r_st])
    P.add("dve", lambda e: e.tensor_scalar(out=stat[:, 35 + c:36 + c], in0=stat[:, 34 + c:35 + c], scalar1=1.0 / D, scalar2=EPS,
                                           op0=ALU.mult, op1=ALU.add), reads=[r_st], writes=[r_st])
    P.add("pool", lambda e: e.tensor_tensor(out=stat[:, 48 + ci:49 + ci], in0=stat[:, 35 + c:36 + c], in1=mhalf[:, 0:1], op=ALU.pow),
          reads=[r_st, r_const], writes=[r_st])
    for hf, (src, rps) in enumerate(srcs):
        P.add("dve", lambda e, src=src, hf=hf: e.scalar_tensor_tensor(out=src, in0=src, scalar=stat[:, 48 + ci:49 + ci],
                                                                   in1=gbc[:, hf * 512:(hf + 1) * 512], op0=ALU.mult, op1=ALU.mult),
              reads=[r_st, r_bc], writes=[rps])
        P.add("dve", lambda e, src=src, hf=hf: e.tensor_tensor(out=xt[:, hf * 512:(hf + 1) * 512], in0=src, in1=xt[:, hf * 512:(hf + 1) * 512], op=ALU.add),
              reads=[rps, r_x], writes=[r_out])


_NC = [None]


def _prep(x_prompt, x_sample, c_prompt, c_sample, w_ada, b_ada, g_mix_pre, g_mix_post, g_ffn_pre, g_ffn_post,
          w_in, rpb, w_branch_a, w_branch_b, w_out, w_up, conv_w, conv_b, w_down):
    f = lambda a: np.ascontiguousarray(np.asarray(a, dtype=np.float32))
    xs = np.concatenate([f(x_prompt), f(x_sample)], axis=0)
    cs = np.concatenate([f(c_prompt), f(c_sample)], axis=0)
    COS, SIN, perm = _rope_tables()
    rr, cr, m1, m2 = _na_index()
    consts = np.zeros((128, NCC), np.float32)
    consts[:, C_ID:C_ID + 128] = np.eye(128, dtype=np.float32)
    consts[:, C_PERM:C_PERM + 128] = perm
    xx = np.arange(384)[None, :]
    pp = np.arange(128)[:, None]
    consts[:, C_T:C_T + 384] = np.where(np.abs(xx - pp - 64) <= 64, 0.0, NEG)
    consts[:, C_M1:C_M1 + 896] = np.where(m1, 1e30, NEG)
    consts[:, C_A1:C_A1 + 896] = np.where(m2, 1e30, NEG)
    consts[:, C_COS:C_COS + S] = COS
    consts[:, C_SIN:C_SIN + S] = SIN
    consts[:, C_ONES:C_ONES + 128] = 1.0
    G = np.ascontiguousarray(f(rpb)[0][:, rr, cr])
    spb = np.zeros((128, NSP), np.float32)
    spb[:, SP_GPRE:SP_GPRE + 8] = _fm(f(g_mix_pre)[0], 8)
    spb[:, SP_GPOST:SP_GPOST + 8] = _fm(f(g_mix_post)[0], 8)
    spb[:, SP_FPRE:SP_FPRE + 8] = _fm(f(g_ffn_pre)[0], 8)
    spb[:, SP_FPOST:SP_FPOST + 8] = _fm(f(g_ffn_post)[0], 8)
    spb[:, SP_BADA:SP_BADA + 48] = _fm(f(b_ada)[0], 48)
    cw = f(conv_w)[0]
    spb[:, SP_CW0:SP_CW0 + 44] = _fm(cw[0], 44)
    spb[:, SP_CW1:SP_CW1 + 44] = _fm(cw[1], 44)
    spb[:, SP_CW2:SP_CW2 + 44] = _fm(cw[2], 44)
    spb[:, SP_CB:SP_CB + 44] = _fm(f(conv_b)[0], 44)
    shared = {"consts": consts, "G": G, "w_ada": f(w_ada)[0], "w_in": f(w_in)[0], "w_ba": f(w_branch_a)[0],
              "w_bb": f(w_branch_b)[0], "w_out": f(w_out)[0], "w_up": f(w_up)[0], "w_down": f(w_down)[0]}
    in_maps = []
    for i in range(8):
        sp_i = spb.copy()
        ci = cs[3 * i:3 * i + 3]
        sp_i[:, SP_C:SP_C + 24] = ci.reshape(3, 8, 128).transpose(2, 1, 0).reshape(128, 24)
        m = dict(shared)
        m["x"] = np.ascontiguousarray(xs[3 * i:3 * i + 3])
        m["sp"] = sp_i
        in_maps.append(m)
    return in_maps


def kernel(**inputs):
    in_maps = _prep(**inputs)
    if _NC[0] is None:
        _NC[0] = build()
    nc = _NC[0]
    res = run_bass_kernel_spmd(nc, in_maps, core_ids=list(range(8)))
    ys = np.concatenate([np.asarray(r["y"], dtype=np.float32) for r in res.results], axis=0)
    return ys[:8].copy(), ys[8:].copy()
```

```python
import contextlib
import numpy as np
import concourse.bass as bass
import concourse.mybir as mybir
from concourse.bass_utils import run_bass_kernel_spmd

F32 = mybir.dt.float32
BF16 = mybir.dt.bfloat16
AF = mybir.ActivationFunctionType
ALU = mybir.AluOpType

ENGS = ("pe", "act", "dve", "pool", "sp")
NSLOT = 8
S = 2048
D = 1024
NSEQ = 3
NEG = -240000.0
EPS = 1e-6


class Res:
    __slots__ = ("w", "r")

    def __init__(self):
        self.w = None
        self.r = []


class Ins:
    __slots__ = ("eng", "fn", "deps", "sig", "cnt", "dma", "slot", "slotcnt", "qkey")

    def __init__(self, eng, fn, dma):
        self.eng = eng
        self.fn = fn
        self.deps = []
        self.sig = False
        self.cnt = 0
        self.dma = dma
        self.slot = 0
        self.slotcnt = 0
        self.qkey = eng


class Prog:
    def __init__(self, nc):
        self.nc = nc
        self.q = {e: [] for e in ENGS}
        self.ndma = {}
        self.lastdma = {}

    def add(self, eng, fn, reads=(), writes=(), dma=False, group=None):
        x = Ins(eng, fn, dma)
        deps = []
        for r in reads:
            if r.w is not None:
                deps.append(r.w)
        for r in writes:
            if r.w is not None:
                deps.append(r.w)
            deps.extend(r.r)
        for r in reads:
            if not dma:
                r.r = [y for y in r.r if not (y.eng == eng and not y.dma)]
            r.r.append(x)
        for r in writes:
            r.w = x
            r.r = []
        if dma:
            qk = eng if group is None else eng + ":" + group
            x.qkey = qk
            n = self.ndma.get(qk, 0)
            x.slot = n % NSLOT
            x.slotcnt = n // NSLOT + 1
            self.ndma[qk] = n + 1
            prev = self.lastdma.setdefault(qk, {}).get(x.slot)
            if prev is not None:
                deps.append(prev)
            self.lastdma[qk][x.slot] = x
        seen = set()
        for d in deps:
            if d is x or id(d) in seen:
                continue
            seen.add(id(d))
            if (not d.dma) and d.eng == "pe" and eng == "pe" and not dma:
                continue
            x.deps.append(d)
            if not d.dma:
                d.sig = True
        self.q[eng].append(x)
        return x

    def barrier(self):
        lasts = []
        for e in ENGS:
            comp = [y for y in self.q[e] if not y.dma and y.fn is not None]
            if comp:
                comp[-1].sig = True
                lasts.append(comp[-1])
            for _, y in self.lastdma.get(e, {}).items():
                lasts.append(y)
        for e in ENGS:
            x = Ins(e, None, False)
            x.deps = list(lasts)
            self.q[e].append(x)

    def emit(self):
        nc = self.nc
        for e in ENGS:
            c = 0
            for x in self.q[e]:
                if x.dma or x.fn is None:
                    continue
                if x.sig:
                    c += 1
                    x.cnt = c
        with contextlib.ExitStack() as st:
            csem = {e: st.enter_context(nc.semaphore("c_" + e)) for e in ENGS}
            dsem = {qk: [st.enter_context(nc.semaphore("d_%s%d" % (qk.replace(":", "_"), i))) for i in range(NSLOT)]
                    for qk in self.ndma}
            block = st.enter_context(nc.Block())
            engobj = {"pe": "tensor", "act": "scalar", "dve": "vector", "pool": "gpsimd", "sp": "sync"}

            def run(e, eng):
                waited = {}
                for x in self.q[e]:
                    for d in x.deps:
                        if d.dma:
                            key = (d.qkey, d.slot)
                            val = d.slotcnt
                            if waited.get(key, 0) >= val:
                                continue
                            waited[key] = val
                            eng.wait_ge(dsem[d.qkey][d.slot], 16 * val)
                        else:
                            key = d.eng
                            val = d.cnt
                            if waited.get(key, 0) >= val:
                                continue
                            waited[key] = val
                            eng.wait_ge(csem[d.eng], val)
                    if x.fn is None:
                        continue
                    bi = x.fn(eng)
                    if x.dma:
                        bi.then_inc(dsem[x.qkey][x.slot], 16)
                    elif x.sig:
                        bi.then_inc(csem[e], 1)

            for e in ENGS:
                if self.q[e]:
                    getattr(block, engobj[e])(lambda eng, e=e: run(e, eng))


def _rope_tables():
    half = 8
    inv = np.power(np.float32(500000.0), -np.arange(half, dtype=np.float32) / np.float32(half)).astype(np.float32)
    pos = np.arange(S, dtype=np.float32)
    ang = (pos[:, None] * inv[None, :]).astype(np.float32)
    c = np.cos(ang).astype(np.float32).T
    s = np.sin(ang).astype(np.float32).T
    COS = np.ones((128, S), np.float32)
    SIN = np.zeros((128, S), np.float32)
    for b in (0, 64):
        COS[b:b + 8] = c
        COS[b + 8:b + 16] = c
        SIN[b:b + 8] = -s
        SIN[b + 8:b + 16] = s
    perm = np.zeros((128, 128), np.float32)
    for b in (0, 64):
        for i in range(8):
            perm[b + 8 + i, b + i] = 1.0
            perm[b + i, b + 8 + i] = 1.0
    return COS, SIN, perm


def _na_index():
    Rk = np.arange(128) // 64
    ck = np.arange(128) % 64
    rho = np.arange(-6, 8)
    c = np.arange(64)
    row_rel = Rk[:, None, None] - rho[None, :, None] + 7 + 0 * c[None, None, :]
    col_rel = ck[:, None, None] - c[None, None, :] + 15 + 0 * rho[None, :, None]
    ok = (row_rel >= 0) & (row_rel <= 14) & (col_rel >= 0) & (col_rel <= 30)
    cs = np.clip(c - 8, 0, 48)
    colvalid = (ck[:, None, None] >= cs[None, None, :]) & (ck[:, None, None] < cs[None, None, :] + 16) \
        & (rho[None, :, None] > -100)
    d = Rk[:, None, None] - rho[None, :, None] + 0 * c[None, None, :]
    rowvalid = (d >= -4) & (d <= 3)
    m1 = (ok & colvalid & rowvalid).reshape(128, 896)
    m2 = (ok & colvalid).reshape(128, 896)
    return np.clip(row_rel, 0, 14).reshape(128, 896), np.clip(col_rel, 0, 30).reshape(128, 896), m1, m2


C_ID = 0
C_PERM = 128
C_T = 256
C_M1 = 640
C_A1 = 1536
C_M2 = 2432
C_A2 = 3328
C_COS = 4224
C_SIN = 6272
C_ONES = 8320
NCC = 8448

SP_GPRE, SP_GPOST, SP_FPRE, SP_FPOST, SP_BADA, SP_CW0, SP_CW1, SP_CW2, SP_CB, SP_C = 0, 8, 16, 24, 32, 80, 124, 168, 212, 256
NSP = 280


def _fm(v, nch):
    return np.ascontiguousarray(v.reshape(nch, 128).T)


def build(debug=False):
    nc = bass.Bass("TRN2", target_bir_lowering=False)
    nseq = 1 if debug else NSEQ
    dt = lambda name, shape, kind="ExternalInput": nc.dram_tensor(name, shape, F32, kind=kind)
    x_d = dt("x", [NSEQ, S, D])
    y_d = dt("y", [NSEQ, S, D], "ExternalOutput")
    consts_d = dt("consts", [128, NCC])
    sp_d = dt("sp", [128, NSP])
    G_d = dt("G", [8, 128, 896])
    wada_d = dt("w_ada", [D, 6 * D])
    win_d = dt("w_in", [D, 5120])
    wba_d = dt("w_ba", [512, D])
    wbb_d = dt("w_bb", [512, D])
    wout_d = dt("w_out", [D, D])
    wup_d = dt("w_up", [D, 5632])
    wdn_d = dt("w_down", [2816, D])

    P = Prog(nc)
    A = nc.alloc_sbuf_tensor
    if debug:
        dbg = {
            "hT": nc.dram_tensor("dbg_hT", [128, 8 * S], BF16, kind="ExternalOutput"),
            "h2T": nc.dram_tensor("dbg_h2T", [128, 8 * S], BF16, kind="ExternalOutput"),
            "attA": nc.dram_tensor("dbg_attA", [128, 4 * S], BF16, kind="ExternalOutput"),
            "attB": nc.dram_tensor("dbg_attB", [128, 4 * S], BF16, kind="ExternalOutput"),
            "x1": nc.dram_tensor("dbg_x1", [128, 16 * D], F32, kind="ExternalOutput"),
            "modT": nc.dram_tensor("dbg_modT", [128, 144], F32, kind="ExternalOutput"),
            "gt1bc": nc.dram_tensor("dbg_gt1bc", [128, D], F32, kind="ExternalOutput"),
            "QK": nc.dram_tensor("dbg_QK", [128, 4096], BF16, kind="ExternalOutput"),
            "V": nc.dram_tensor("dbg_V", [128, 12288], BF16, kind="ExternalOutput"),
        }

    def dump(name, src):
        if debug:
            P.barrier()
            P.add("sp", lambda e: e.dma_start(out=dbg[name].ap(), in_=src), dma=True)
            P.barrier()

    identf = A("identf", [128, 128], F32)
    onesf = A("onesf", [128, 128], F32)
    identb = A("identb", [128, 128], BF16)
    permb = A("permb", [128, 128], BF16)
    Tb = A("Tb", [128, 384], BF16)
    csb = A("csb", [128, 2 * S], BF16)
    cosb, sinb = csb[:, 0:S], csb[:, S:2 * S]
    spt = A("spt", [128, NSP], F32)
    modT = A("modT", [128, 144], F32)
    vec = A("vec", [128, 64], F32)
    gt1bc = A("gt1bc", [128, D], F32)
    gt2bc = A("gt2bc", [128, D], F32)
    stat = A("stat", [128, 64], F32)
    NW = 2
    wring = [A("wr%d" % i, [128, 8 * 512], BF16) for i in range(NW)]
    hT = A("hT", [128, 8 * S], BF16)
    attA = A("attA", [128, 4 * S], BF16)
    attB = A("attB", [128, 4 * S], BF16)
    RA = A("RA", [128, 16 * D], F32)
    RAb = RA.bitcast(BF16)
    VA_OFF = 0
    QT_OFF = 12288
    KT_OFF = 14336
    QRAW_OFF = 16384
    EXP_OFF = 17408
    QO_OFF = 30720
    ACC_OFF_F = 10240
    RDEN_OFF_F = 12288
    TMP_OFF_F = 14336
    RC = A("RC", [128, 9728], F32)
    RCb = RC.bitcast(BF16)
    RB = RC[:, 5632:9728]
    gT = RCb[:, 0:11264]
    badd = RCb[:, 0:14336]
    attAf = attA.bitcast(F32)
    t1v, t1g, sgs = attAf[:, 0:512], attAf[:, 512:1024], attAf[:, 1024:2048]
    t1v_c, t1g_c, sgs_c = RB[:, 0:512], RB[:, 512:1024], RB[:, 1024:2048]
    h2halo = A("h2halo", [128, 8 * 8], BF16)
    Etab = A("Etab", [128, 44 * 6], F32)
    junk = A("junk", [128, D], BF16)

    pb = [nc.alloc_psum_tensor("pb%d" % i, [128, 512], F32) for i in range(8)]
    rb = [Res() for _ in range(8)]
    bank_i = [0]
    ffb = [0]
    ffs = [0]
    pend_epi = [None]

    def nb():
        i = bank_i[0] % 8
        bank_i[0] += 1
        return pb[i], rb[i]

    def ap(t, rowlen, p0, np_, off, dims):
        return bass.AP(t, p0 * rowlen + off, [[rowlen, np_]] + [list(d) for d in dims])

    HT_L, ATT_L, RAB_L, RAF_L = 8 * S, 4 * S, 32768, 16 * D

    r_const = Res()
    r_t1v, r_t1g, r_sg0, r_sg1, r_halo, r_ta_p, r_tb_p, r_stln, r_badd = [Res() for _ in range(9)]
    r_qraw = [Res(), Res()]
    r_xb = [Res() for _ in range(4)]
    r_x1all, r_xn2, r_mgp = Res(), Res(), Res()
    r_Qp, r_Kp, r_Vp, r_accp, r_rdp = [Res() for _ in range(5)]
    r_pc = [(Res(), Res()), (Res(), Res())]
    r_ff = [(Res(), Res()), (Res(), Res()), (Res(), Res())]
    r_exb = [Res() for _ in range(4)]
    r_hT = [[Res() for _ in range(4)] for _ in range(8)]
    class Slot:
        def __init__(self, t, rowlen, off):
            self.t, self.rowlen, self.off, self.res = t, rowlen, off, Res()

    class Ring:
        def __init__(self, slots):
            self.slots, self.i = slots, 0

        def next(self):
            sl = self.slots[self.i % len(self.slots)]
            self.i += 1
            return sl

    sl_w0, sl_w1 = Slot(wring[0], 4096, 0), Slot(wring[1], 4096, 0)
    ring0 = Ring([sl_w0, sl_w1])
    sl_cs = Slot(csb, 2 * S, 0)
    r_cs = sl_cs.res
    ringC = Ring([sl_w0, sl_w1, sl_cs])
    ringAB = Ring([Slot(RCb, 19456, 4096), Slot(RCb, 19456, 15360)])
    ringF = Ring([sl_w0, sl_w1, Slot(attB, ATT_L, 0), Slot(attB, ATT_L, 4096)])
    ring_setup = Ring([sl_w0, sl_w1] + ringF.slots[2:4] + ringAB.slots + [sl_cs])

    def wload(srcd, kc, ncols, ring=None):
        src, deps = srcd
        sl = (ring or ring0).next()
        dst = bass.AP(sl.t, sl.off, [[sl.rowlen, 128], [ncols, kc], [1, ncols]])
        q_ = "sp" if deps else "pool"
        P.add(q_, lambda e, dst=dst, src=src: e.dma_start(out=dst, in_=src), reads=deps, writes=[sl.res], dma=True)
        return sl, sl.res

    def wsl(sl, k, ncols, c0, n):
        return bass.AP(sl.t, sl.off + k * ncols + c0, [[sl.rowlen, 128], [1, n]])

    wconv = {}

    cv_pending = []

    def convert(w_d, nrows, ncols, name):
        wb_d = nc.dram_tensor(name + "_bf", [nrows, ncols], BF16, kind="Internal")
        rs = []
        for c0 in range(0, ncols, 512):
            r = Res()
            cv_pending.append(lambda c0=c0, r=r, wb_d=wb_d, w_d=w_d: P.add(
                "pool", lambda e: e.dma_start(out=wb_d.ap()[:, c0:c0 + 512], in_=w_d.ap()[:, c0:c0 + 512]), writes=[r], dma=True, group="cv"))
            rs.append(r)
        wconv[id(w_d)] = (wb_d, rs)

    def convert_step(n):
        for _ in range(n):
            if cv_pending:
                cv_pending.pop(0)()

    def wsrc(w_d, nrows_k, c0, ncols, k0=0, kc=None, rowlen=None):
        kc = nrows_k if kc is None else kc
        deps = []
        if id(w_d) in wconv:
            w_d, rs = wconv[id(w_d)]
            deps = rs[c0 // 512:(c0 + ncols - 1) // 512 + 1]
        return bass.AP(w_d, (k0 * 128) * rowlen + c0, [[rowlen, 128], [128 * rowlen, kc], [1, ncols]]), deps

    P.add("sp", lambda e: e.dma_start(out=identf[:], in_=consts_d.ap()[:, C_ID:C_ID + 128]), writes=[r_const], dma=True)
    P.add("sp", lambda e: e.dma_start(out=onesf[:], in_=consts_d.ap()[:, C_ONES:C_ONES + 128]), writes=[r_const], dma=True)
    P.add("sp", lambda e: e.dma_start(out=spt[:], in_=sp_d.ap()), writes=[r_const], dma=True)
    for (t, c0, n) in ((identb, C_ID, 128), (permb, C_PERM, 128), (Tb, C_T, 384)):
        P.add("pool", lambda e, t=t, c0=c0, n=n: e.dma_start(out=t[:], in_=consts_d.ap()[:, c0:c0 + n]),
              writes=[r_const], dma=True)
    P.add("dve", lambda e: e.memset(RAb[:, 0:12288], 1.0), writes=[r_const])
    P.barrier()
    attBf = attB.bitcast(F32)
    r_gst = [Res() for _ in range(4)]
    r_dg_cur = [None]
    r_cap = Res()

    def gen_badd_part(part):
        if part == 0:
            P.add("sp", lambda e: e.dma_start(out=attAf[:, 0:1792], in_=consts_d.ap()[:, C_M1:C_M1 + 1792]), writes=[r_cap], dma=True)
        for h in (2 * part, 2 * part + 1):
            g = attBf[:, (h % 4) * 1024:(h % 4) * 1024 + 896]
            rg = r_gst[h % 4]
            P.add("sp", lambda e, g=g, h=h: e.dma_start(out=g, in_=G_d.ap()[h]), writes=[rg], dma=True)
            for v in range(2):
                o = badd[:, (h * 2 + v) * 896:(h * 2 + v + 1) * 896]
                P.add("dve", lambda e, o=o, g=g, v=v: e.scalar_tensor_tensor(out=o, in0=g, scalar=8.0, in1=attAf[:, v * 896:(v + 1) * 896],
                                                                          op0=ALU.mult, op1=ALU.min), reads=[rg, r_cap], writes=[r_badd, r_dg_cur[0]])

    silub = A("silub", [128, 24], BF16)
    r_sil = Res()
    P.add("act", lambda e: e.activation(out=silub[:], in_=spt[:, SP_C:SP_C + 24], func=AF.Silu), writes=[r_sil])
    modps, r_modps = pb[0], rb[0]
    for tile in range(12):
        wt, wr_ = wload(wsrc(wada_d, 8, tile * 512, 512, rowlen=6 * D), 8, 512, ring_setup)
        for sub in range(4):
            ch = tile * 4 + sub
            for k in range(8):
                P.add("pe", lambda e, wt=wt, sub=sub, k=k, ch=ch: e.matmul(
                    modps[:, ch * 3:ch * 3 + 3], lhsT=wsl(wt, k, 512, sub * 128, 128),
                    rhs=bass.AP(silub, k * 3, [[24, 128], [1, 3]]), start=(k == 0), stop=(k == 7), skip_group_check=True),
                    reads=[wr_, r_sil], writes=[r_modps])
    for (w_d_, nr_, ncl_, nm_) in ((win_d, D, 5120, "w_in"), (wba_d, 512, D, "w_ba"), (wbb_d, 512, D, "w_bb"), (wout_d, D, D, "w_out"),
                                   (wup_d, D, 5632, "w_up"), (wdn_d, 2816, D, "w_down")):
        convert(w_d_, nr_, ncl_, nm_)
    convert_step(10)
    r_mod = Res()
    P.add("dve", lambda e: e.tensor_tensor(
        out=bass.AP(modT, 0, [[144, 128], [3, 48], [1, 3]]), in0=bass.AP(modps, 0, [[512, 128], [3, 48], [1, 3]]),
        in1=bass.AP(spt, SP_BADA, [[NSP, 128], [1, 48], [0, 3]]), op=ALU.add), reads=[r_modps, r_const], writes=[r_mod])
    P.barrier()

    def modv(part, s):
        return bass.AP(modT, part * 24 + s, [[144, 128], [3, 8]])

    r_lnst = [(Res(), Res(), Res()), (Res(), Res(), Res())]
    _rst[0] = [Res() for _ in range(5)]
    r_junk_all = _rst[0][4]

    def ln_stats(src_tiles, src_res, tcn):
        r_ss, r_rs, r_rs2 = r_lnst[tcn % 2]
        so = 4 * (tcn % 2)
        for tt in range(4):
            P.add("act", lambda e, tt=tt: e.activation(out=junk[:], in_=src_tiles[tt], func=AF.Square,
                                                      accum_out=stat[:, so + tt:so + tt + 1]), reads=[src_res], writes=[r_ss, r_junk_all])
        P.add("dve", lambda e: e.tensor_scalar(out=stat[:, 8 + so:12 + so], in0=stat[:, so:so + 4], scalar1=1.0 / D, scalar2=EPS,
                                               op0=ALU.mult, op1=ALU.add), reads=[r_ss], writes=[r_rs])
        P.add("pool", lambda e: e.tensor_tensor(out=stat[:, 16 + so:20 + so], in0=stat[:, 8 + so:12 + so], in1=mhalf[:, 0:4], op=ALU.pow),
              reads=[r_rs, r_const], writes=[r_rs2])

    def ln_apply(src_tiles, src_res, tcn, sc_ap, sh_ap, inplace_dst, r_xn):
        r_ss, r_rs, r_rs2 = r_lnst[tcn % 2]
        so = 4 * (tcn % 2)
        for tt in range(4):
            P.add("dve", lambda e, tt=tt: e.tensor_scalar(out=inplace_dst[tt], in0=src_tiles[tt], scalar1=stat[:, 16 + so + tt:17 + so + tt],
                                                          scalar2=None, op0=ALU.mult), reads=[src_res, r_rs2], writes=[r_xn])
        for j in range(8):
            ps, rps = nb()
            for tt in range(4):
                P.add("pe", lambda e, ps=ps, tt=tt, j=j: e.transpose(out=ps[:, tt * 128:(tt + 1) * 128],
                                                                     in_=inplace_dst[tt][:, j * 128:(j + 1) * 128], identity=identf[:]),
                      reads=[r_xn, r_const], writes=[rps])
            o = hT[:, j * S + tcn * 512: j * S + tcn * 512 + 512]
            if j % 2 == 0:
                P.add("act", lambda e, o=o, ps=ps, j=j: e.activation(out=o, in_=ps[:], func=AF.Identity, scale=sc_ap[:, j:j + 1],
                                                                     bias=sh_ap[:, j:j + 1]), reads=[rps, r_vec], writes=[r_hT[j][tcn]])
            else:
                P.add("dve", lambda e, o=o, ps=ps, j=j: e.tensor_scalar(out=o, in0=ps[:], scalar1=sc_ap[:, j:j + 1], scalar2=sh_ap[:, j:j + 1],
                                                                        op0=ALU.mult, op1=ALU.add), reads=[rps, r_vec], writes=[r_hT[j][tcn]])

    epsb = A("epsb", [128, 1], F32)
    mhalf = A("mhalf", [128, 4], F32)
    P.add("dve", lambda e: e.memset(epsb[:], EPS), writes=[r_const])
    P.add("dve", lambda e: e.memset(mhalf[:], -0.5), writes=[r_const])
    r_vec = Res()
    hT_all = [r_hT[j][t] for j in range(8) for t in range(4)]

    def x1tile(i):
        return RA[:, i * D:(i + 1) * D]

    def xbuf(tcn):
        return [bass.AP(RA, tcn * 4096 + tt * 1024, [[RAF_L, 128], [1, 1024]]) for tt in range(4)]

    for s in range(nseq):
        for tcn in range(4):
            dstx = bass.AP(RA, tcn * 4096, [[RAF_L, 128], [1024, 4], [1, 1024]])
            srcx = bass.AP(x_d, (s * S + tcn * 512) * D, [[D, 128], [128 * D, 4], [1, D]])
            P.add("sp", lambda e, dstx=dstx, srcx=srcx: e.dma_start(out=dstx, in_=srcx), writes=[r_xb[tcn]], dma=True)
        P.add("dve", lambda e, s=s: e.scalar_tensor_tensor(out=vec[:, 0:8], in0=modv(1, s), scalar=1.0, in1=spt[:, SP_GPRE:SP_GPRE + 8],
                                                           op0=ALU.add, op1=ALU.mult), reads=[r_mod, r_const], writes=[r_vec])
        P.add("dve", lambda e, s=s: e.tensor_copy(out=vec[:, 8:16], in_=modv(0, s)), reads=[r_mod], writes=[r_vec])
        P.add("dve", lambda e, s=s: e.tensor_tensor(out=vec[:, 16:24], in0=modv(2, s), in1=spt[:, SP_GPOST:SP_GPOST + 8], op=ALU.mult),
              reads=[r_mod, r_const], writes=[r_vec])
        P.add("dve", lambda e, s=s: e.scalar_tensor_tensor(out=vec[:, 24:32], in0=modv(4, s), scalar=1.0, in1=spt[:, SP_FPRE:SP_FPRE + 8],
                                                           op0=ALU.add, op1=ALU.mult), reads=[r_mod, r_const], writes=[r_vec])
        P.add("dve", lambda e, s=s: e.tensor_copy(out=vec[:, 32:40], in_=modv(3, s)), reads=[r_mod], writes=[r_vec])
        P.add("dve", lambda e, s=s: e.tensor_tensor(out=vec[:, 40:48], in0=modv(5, s), in1=spt[:, SP_FPOST:SP_FPOST + 8], op=ALU.mult),
              reads=[r_mod, r_const], writes=[r_vec])
        r_bc = Res()
        r_dg = Res()
        r_dg_cur[0] = r_dg
        for (c0, dst) in ((16, gt1bc), (40, gt2bc)):
            dg = RB
            P.add("dve", lambda e, c0=c0: e.tensor_tensor(
                out=bass.AP(RC, 5632, [[9728, 128], [128, 8], [1, 128]]), in0=bass.AP(identf, 0, [[128, 128], [0, 8], [1, 128]]),
                in1=bass.AP(vec, c0, [[64, 128], [1, 8], [0, 128]]), op=ALU.mult), reads=[r_vec, r_const], writes=[r_dg])
            for hf in range(2):
                ps, rps = nb()
                P.add("pe", lambda e, ps=ps, hf=hf: e.matmul(ps[:], lhsT=onesf[:], rhs=RB[:, hf * 512:(hf + 1) * 512], start=True, stop=True),
                      reads=[r_dg, r_const], writes=[rps])
                P.add("act", lambda e, ps=ps, hf=hf, dst=dst: e.activation(out=dst[:, hf * 512:(hf + 1) * 512], in_=ps[:], func=AF.Identity),
                      reads=[rps], writes=[r_bc])

        ln_stats(xbuf(0), r_xb[0], 0)
        for tcn in range(4):
            if tcn + 1 < 4:
                ln_stats(xbuf(tcn + 1), r_xb[tcn + 1], tcn + 1)
            ln_apply(xbuf(tcn), r_xb[tcn], tcn, vec[:, 0:8], vec[:, 8:16], xbuf(tcn), r_xb[tcn])
            gen_badd_part(tcn)
        P.barrier()
        dump('hT', hT[:])
        dump('modT', modT[:])
        dump('gt1bc', gt1bc[:])
        P.add("dve", lambda e: e.memset(bass.AP(RAb, 64, [[RAB_L, 128], [256, 48], [1, 128]]), 1.0), writes=[r_const])
        P.add("dve", lambda e: e.memset(bass.AP(RAb, 64 * RAB_L + QT_OFF, [[RAB_L, 64], [1, S]]), 0.0), writes=[r_const])
        P.add("dve", lambda e: e.memset(bass.AP(RAb, QO_OFF, [[RAB_L, 64], [1, S]]), 0.0), writes=[r_const])
        P.barrier()

        P.add("pool", lambda e: e.dma_start(out=csb[:], in_=consts_d.ap()[:, C_COS:C_COS + 2 * S]), writes=[r_cs], dma=True)
        for mixer in range(2):
            if mixer == 1:
                dump('QK', RAb[:, QT_OFF:QT_OFF + 4096])
                dump('V', RAb[:, 0:12288])
            att = attA if mixer == 0 else attB
            qcol0 = 0 if mixer == 0 else 1536
            kcol0 = 512 if mixer == 0 else 2048
            vcol0 = 1024 if mixer == 0 else 2560
            norders = 3 if mixer == 0 else 1
            for hp in range(4):
                wt, wr_ = wload(wsrc(win_d, 8, vcol0 + hp * 128, 128, rowlen=5120), 8, 128)
                r_V = r_Vp
                for tcn in range(4):
                    ps, rps = nb()
                    for k in range(8):
                        P.add("pe", lambda e, ps=ps, k=k, tcn=tcn, wt=wt: e.matmul(
                            ps[:], lhsT=wsl(wt, k, 128, 0, 128), rhs=hT[:, k * S + tcn * 512:k * S + tcn * 512 + 512],
                            start=(k == 0), stop=(k == 7)), reads=hT_all + [wr_], writes=[rps])
                    P.add("act", lambda e, ps=ps, tcn=tcn: e.activation(out=RAb[:, EXP_OFF + tcn * 512:EXP_OFF + tcn * 512 + 512], in_=ps[:], func=AF.Identity),
                          reads=[rps], writes=[r_exb[tcn]])
                for o in range(norders):
                    for g4 in range(4):
                        ps, rps = nb()
                        psb16 = ps.bitcast(BF16)
                        for q in range(4):
                            i = g4 * 4 + q
                            if o == 0:
                                off, st_ = 128 * i, 1
                            elif o == 1:
                                off, st_ = 512 * (i % 4) + i // 4, 4
                            else:
                                off, st_ = i, 16
                            P.add("pe", lambda e, psb16=psb16, q=q, off=off, st_=st_: e.transpose(
                                out=psb16[:, q * 128:(q + 1) * 128], in_=bass.AP(RAb, EXP_OFF + off, [[RAB_L, 128], [st_, 128]]), identity=identb[:]),
                                reads=r_exb + [r_const], writes=[rps])
                        dst = bass.AP(RAb, VA_OFF + (o * 16 + g4 * 4) * 256, [[RAB_L, 128], [256, 4], [192, 2], [1, 64]])
                        srcp = bass.AP(psb16, 0, [[1024, 128], [128, 4], [64, 2], [1, 64]])
                        eng_ = "act" if g4 % 2 == 0 else "dve"
                        if eng_ == "act":
                            P.add("act", lambda e, dst=dst, srcp=srcp: e.activation(out=dst, in_=srcp, func=AF.Identity), reads=[rps], writes=[r_V])
                        else:
                            P.add("dve", lambda e, dst=dst, srcp=srcp: e.tensor_copy(out=dst, in_=srcp), reads=[rps], writes=[r_V])
                r_Q, r_K = r_Qp, r_Kp
                convert_step(3)
                for (col0, toff, rres) in ((qcol0, QT_OFF, r_Q), (kcol0, KT_OFF, r_K)):
                    wt, wr_ = wload(wsrc(win_d, 8, col0 + hp * 128, 128, rowlen=5120), 8, 128)
                    for tcn in range(4):
                        ps, rps = nb()
                        for k in range(8):
                            P.add("pe", lambda e, ps=ps, k=k, tcn=tcn, wt=wt: e.matmul(
                                ps[:], lhsT=wsl(wt, k, 128, 0, 128), rhs=hT[:, k * S + tcn * 512:k * S + tcn * 512 + 512],
                                start=(k == 0), stop=(k == 7)), reads=hT_all + [wr_], writes=[rps])
                        dstq = RAb[:, toff + tcn * 512: toff + tcn * 512 + 512]
                        isq = (toff == QT_OFF)
                        dq_e = bass.AP(RAb, QT_OFF + tcn * 512, [[RAB_L, 64], [1, 512]])
                        dq_o = bass.AP(RAb, 64 * RAB_L + QO_OFF + tcn * 512, [[RAB_L, 64], [1, 512]])
                        if mixer == 1 and isq:
                            P.add("act", lambda e, dq_e=dq_e, ps=ps: e.activation(out=dq_e, in_=ps[0:64, :], func=AF.Identity),
                                  reads=[rps], writes=[rres])
                            P.add("act", lambda e, dq_o=dq_o, ps=ps: e.activation(out=dq_o, in_=ps[64:128, :], func=AF.Identity),
                                  reads=[rps], writes=[rres])
                        elif mixer == 1:
                            P.add("act", lambda e, dstq=dstq, ps=ps: e.activation(out=dstq, in_=ps[:], func=AF.Identity),
                                  reads=[rps], writes=[rres])
                        else:
                            qraw = RAb[:, QRAW_OFF + (tcn % 2) * 512: QRAW_OFF + (tcn % 2) * 512 + 512]
                            r_qr = r_qraw[tcn % 2]
                            P.add("act", lambda e, qraw=qraw, ps=ps: e.activation(out=qraw, in_=ps[:], func=AF.Identity),
                                  reads=[rps], writes=[r_qr])
                            ps2, rps2 = nb()
                            P.add("pe", lambda e, ps2=ps2, qraw=qraw: e.matmul(ps2[:], lhsT=permb[:], rhs=qraw, start=True, stop=True),
                                  reads=[r_qr, r_const], writes=[rps2])
                            ta = RA[:, TMP_OFF_F: TMP_OFF_F + 512]
                            tb_ = RA[:, TMP_OFF_F + 512: TMP_OFF_F + 1024]
                            r_ta, r_tb = r_ta_p, r_tb_p
                            P.add("dve", lambda e, ta=ta, ps2=ps2, tcn=tcn: e.tensor_tensor(out=ta, in0=ps2[:], in1=sinb[:, tcn * 512:(tcn + 1) * 512],
                                                                                          op=ALU.mult), reads=[rps2, r_cs], writes=[r_ta])
                            P.add("dve", lambda e, tb_=tb_, qraw=qraw, tcn=tcn: e.tensor_tensor(out=tb_, in0=qraw, in1=cosb[:, tcn * 512:(tcn + 1) * 512],
                                                                                              op=ALU.mult), reads=[r_qr, r_cs], writes=[r_tb])
                            if isq:
                                P.add("dve", lambda e, dq_e=dq_e, ta=ta, tb_=tb_: e.tensor_tensor(out=dq_e, in0=ta[0:64, :], in1=tb_[0:64, :], op=ALU.add),
                                      reads=[r_ta, r_tb], writes=[rres])
                                P.add("dve", lambda e, dq_o=dq_o, ta=ta, tb_=tb_: e.tensor_tensor(out=dq_o, in0=ta[64:128, :], in1=tb_[64:128, :], op=ALU.add),
                                      reads=[r_ta, r_tb], writes=[rres])
                            else:
                                P.add("dve", lambda e, dstq=dstq, ta=ta, tb_=tb_: e.tensor_tensor(out=dstq, in0=ta, in1=tb_, op=ALU.add),
                                      reads=[r_ta, r_tb], writes=[rres])
                for hl in range(2):
                    h = hp * 2 + hl
                    b = 64 * hl
                    accb = [pb[i] for i in range(4)]
                    accr = [rb[i] for i in range(4)]
                    scb = [(pb[4 + i], rb[4 + i]) for i in range(4)]
                    sci = [0]
                    r_acc = r_accp
                    acc_sb = lambda c0, st_, n, p0=0, np_=128: bass.AP(RA, p0 * RAF_L + ACC_OFF_F + c0, [[RAF_L, np_], [st_, n]])

                    LOOK = 2
                    pipe = []

                    def qk_part(units):
                        slot = sci[0] % 4
                        sci[0] += 1
                        ps, rps = scb[slot]
                        col = 0
                        for (koff, kst, vtile, segs) in units:
                            lk = bass.AP(RAb, KT_OFF + koff, [[RAB_L, 128], [kst, 128]])
                            for (qoff, qst, nq, mrhs, bk, ac0) in segs:
                                rq = bass.AP(RAb, (QT_OFF if hl == 0 else QO_OFF) + qoff, [[RAB_L, 128], [qst, nq]])
                                P.add("pe", lambda e, ps=ps, col=col, nq=nq, lk=lk, rq=rq: e.matmul(
                                    ps[:, col:col + nq], lhsT=lk, rhs=rq, start=True, stop=False, skip_group_check=True),
                                    reads=[r_Q, r_K], writes=[rps])
                                P.add("pe", lambda e, ps=ps, col=col, nq=nq, mrhs=mrhs: e.matmul(
                                    ps[:, col:col + nq], lhsT=identb[:], rhs=mrhs, start=False, stop=True, skip_group_check=True),
                                    reads=[r_const, r_badd], writes=[rps])
                                col += nq
                        ex = RAb[:, EXP_OFF + slot * 512: EXP_OFF + slot * 512 + col]
                        P.add("act", lambda e, ex=ex, ps=ps, col=col: e.activation(out=ex, in_=ps[:, 0:col], func=AF.Exp, scale=0.125),
                              reads=[rps], writes=[r_exb[slot]])
                        return (slot, units)

                    def pv_part(ctx, touched):
                        slot, units = ctx
                        col = 0
                        for (koff, kst, vtile, segs) in units:
                            lv = bass.AP(RAb, VA_OFF + vtile * 256 + hl * 128, [[RAB_L, 128], [1, 128]])
                            for (qoff, qst, nq, mrhs, bk, ac0) in segs:
                                first = bk not in touched
                                touched.add(bk)
                                exs = bass.AP(RAb, EXP_OFF + slot * 512 + col, [[RAB_L, 128], [1, nq]])
                                P.add("pe", lambda e, bk=bk, ac0=ac0, nq=nq, lv=lv, exs=exs, first=first: e.matmul(
                                    accb[bk][:, ac0:ac0 + nq], lhsT=lv, rhs=exs, start=first, stop=True, skip_group_check=True),
                                    reads=[r_exb[slot], r_V], writes=[accr[bk]])
                                col += nq

                    def push(units, touched, after=None):
                        pipe.append((qk_part(units), touched, after))
                        if len(pipe) > LOOK:
                            c_, t_, a_ = pipe.pop(0)
                            pv_part(c_, t_)
                            if a_ is not None:
                                a_()

                    def flush():
                        while pipe:
                            c_, t_, a_ = pipe.pop(0)
                            pv_part(c_, t_)
                            if a_ is not None:
                                a_()

                    den_p0 = 64 - b
                    if mixer == 0:
                        def merge(o):
                            for bk in range(4):
                                if o == 0:
                                    dsta = acc_sb(bk * 512, 1, 512)
                                    P.add("act", lambda e, dsta=dsta, bk=bk: e.activation(out=dsta, in_=accb[bk][:], func=AF.Identity),
                                          reads=[accr[bk]], writes=[r_acc])
                                elif o == 1:
                                    dsta = acc_sb(bk, 4, 512)
                                    P.add("dve", lambda e, dsta=dsta, bk=bk: e.tensor_tensor(out=dsta, in0=accb[bk][:], in1=dsta, op=ALU.add),
                                          reads=[accr[bk], r_acc], writes=[r_acc])
                                else:
                                    dsta = bass.AP(RA, ACC_OFF_F + 4 * bk, [[RAF_L, 128], [1, 4], [16, 128]])
                                    srca = bass.AP(accb[bk], 0, [[512, 128], [128, 4], [1, 128]])
                                    P.add("dve", lambda e, dsta=dsta, srca=srca: e.tensor_tensor(out=dsta, in0=srca, in1=dsta, op=ALU.add),
                                          reads=[accr[bk], r_acc], writes=[r_acc])
                            if o == 2:
                                rden = bass.AP(RA, b * RAF_L + RDEN_OFF_F, [[RAF_L, 64], [1, S]])
                                r_rd = r_rdp
                                P.add("act", lambda e, rden=rden, den_p0=den_p0: e.activation(out=rden, in_=acc_sb(0, 1, S, den_p0, 64), func=AF.Ln),
                                      reads=[r_acc], writes=[r_rd])
                                P.add("act", lambda e, rden=rden: e.activation(out=rden, in_=rden, func=AF.Exp, scale=-1.0), writes=[r_rd])
                                dsto = bass.AP(att, b * ATT_L + hp * S, [[ATT_L, 64], [1, S]])
                                P.add("dve", lambda e, dsto=dsto, rden=rden, b=b: e.tensor_tensor(out=dsto, in0=acc_sb(0, 1, S, b, 64), in1=rden, op=ALU.mult),
                                      reads=[r_acc, r_rd], writes=[Res()])

                        for o, d in enumerate((1, 4, 16)):
                            L = S // d
                            touched = set()
                            ulist = []
                            for r in range(d):
                                for blk in range(L // 128):
                                    j0 = 128 * blk
                                    jq0, jq1 = max(0, j0 - 64), min(L, j0 + 192)
                                    segs = []
                                    pos = r * L + jq0
                                    end = r * L + jq1
                                    while pos < end:
                                        e_ = min(end, (pos // 512 + 1) * 512)
                                        jj = pos - r * L
                                        x0 = jj - j0 + 64
                                        segs.append((d * jj + r, d, e_ - pos, Tb[:, x0:x0 + (e_ - pos)], pos // 512, pos % 512))
                                        pos = e_
                                    ulist.append((d * j0 + r, d, o * 16 + (r * L + j0) // 128, segs))
                            groups = [ulist[i:i + 2] for i in range(0, len(ulist), 2)]
                            for gi, units in enumerate(groups):
                                push(units, touched, (lambda o=o: merge(o)) if gi == len(groups) - 1 else None)
                    else:
                        def norm_b():
                            for bk in range(4):
                                rden = bass.AP(RA, b * RAF_L + RDEN_OFF_F + bk * 512, [[RAF_L, 64], [1, 512]])
                                r_rd = r_rdp
                                P.add("act", lambda e, rden=rden, bk=bk, den_p0=den_p0: e.activation(out=rden, in_=accb[bk][den_p0:den_p0 + 64, :], func=AF.Ln),
                                      reads=[accr[bk]], writes=[r_rd])
                                P.add("act", lambda e, rden=rden: e.activation(out=rden, in_=rden, func=AF.Exp, scale=-1.0), writes=[r_rd])
                                dsto = bass.AP(att, b * ATT_L + hp * S + bk * 512, [[ATT_L, 64], [1, 512]])
                                P.add("dve", lambda e, dsto=dsto, rden=rden, bk=bk, b=b: e.tensor_tensor(out=dsto, in0=accb[bk][b:b + 64, :], in1=rden, op=ALU.mult),
                                      reads=[accr[bk], r_rd], writes=[Res()])

                        touched = set()
                        ulist = []
                        for m in range(16):
                            rp_lo, rp_hi = max(4, 2 * m - 3), min(28, 2 * m + 5)
                            r_lo = 0 if rp_lo == 4 else rp_lo
                            r_hi = 31 if rp_hi == 28 else rp_hi
                            r0 = r_lo
                            while r0 <= r_hi:
                                r1 = min(r_hi, (r0 // 8) * 8 + 7)
                                segs = []
                                rr = r0
                                while rr <= r1:
                                    var = 0 if 4 <= rr <= 28 else 1
                                    re_ = rr
                                    while re_ + 1 <= r1 and (0 if 4 <= re_ + 1 <= 28 else 1) == var:
                                        re_ += 1
                                    nq = 64 * (re_ - rr + 1)
                                    rho0 = rr - 2 * m
                                    tb0 = (h * 2 + var) * 896 + (rho0 + 6) * 64
                                    segs.append((64 * rr, 1, nq, badd[:, tb0:tb0 + nq], rr // 8, (64 * rr) % 512))
                                    rr = re_ + 1
                                ulist.append([(128 * m, 1, m, segs)])
                                r0 = r1 + 1
                        for gi, units in enumerate(ulist):
                            push(units, touched, norm_b if gi == len(ulist) - 1 else None)
                    flush()

        convert_step(100)
        P.barrier()
        dump('attA', attA[:])
        dump('attB', attB[:])
        for tcn in range(4):
            r_x = Res()
            dstx = bass.AP(RA, tcn * 4096, [[RAF_L, 128], [1024, 4], [1, 1024]])
            srcx = bass.AP(x_d, (s * S + tcn * 512) * D, [[D, 128], [128 * D, 4], [1, D]])
            P.add("sp", lambda e, dstx=dstx, srcx=srcx: e.dma_start(out=dstx, in_=srcx), writes=[r_x], dma=True)
            r_mg = r_mgp
            pend_add = None
            if tcn == 0:
                wa, wra = wload(wsrc(wba_d, 4, 0, 1024, rowlen=D), 4, 1024, ringAB)
                wb_, wrb = wload(wsrc(wbb_d, 4, 0, 1024, rowlen=D), 4, 1024, ringAB)
            for c in range(8):
                psa, rpsa = nb()
                psb, rpsb = nb()
                for (ps_, w_, wr__, at_) in ((psa, wa, wra, attA), (psb, wb_, wrb, attB)):
                    for k in range(4):
                        P.add("pe", lambda e, ps_=ps_, w_=w_, k=k, at_=at_, tcn=tcn, c=c: e.matmul(
                            ps_[:], lhsT=wsl(w_, k, 1024, c * 128, 128), rhs=at_[:, k * S + tcn * 512:k * S + tcn * 512 + 512],
                            start=(k == 0), stop=(k == 3)), reads=[wr__], writes=[rpsa if ps_ is psa else rpsb])
                if c == 0:
                    wg, wrg = wload(wsrc(win_d, 8, 3072, 512, rowlen=5120), 8, 512, ringC)
                    wg2, wrg2 = wload(wsrc(win_d, 8, 4096, 512, rowlen=5120), 8, 512, ringC)
                if c == 2:
                    wg_n = wload(wsrc(win_d, 8, 3072 + 512, 512, rowlen=5120), 8, 512, ringC)
                if c == 4:
                    wg, wrg = wg_n
                    wg2, wrg2 = wload(wsrc(win_d, 8, 4096 + 512, 512, rowlen=5120), 8, 512, ringC)
                if c == 6:
                    wo = [wload(wsrc(wout_d, 8, 0, 512, rowlen=D), 8, 512, ringC)]
                psg, rpsg = nb()
                psg2, rpsg2 = nb()
                for (ps_, w_, wr__, rr_) in ((psg, wg, wrg, rpsg), (psg2, wg2, wrg2, rpsg2)):
                    for k in range(8):
                        P.add("pe", lambda e, ps_=ps_, w_=w_, k=k, tcn=tcn, c=c: e.matmul(
                            ps_[:], lhsT=wsl(w_, k, 512, (c % 4) * 128, 128), rhs=hT[:, k * S + tcn * 512:k * S + tcn * 512 + 512],
                            start=(k == 0), stop=(k == 7)), reads=hT_all + [wr__], writes=[rr_])
                st_ = c % 2
                sa = RB[:, st_ * 1024: st_ * 1024 + 512]
                sb_ = RB[:, st_ * 1024 + 512: st_ * 1024 + 1024]
                ra_, rb_ = r_pc[st_]
                P.add("act", lambda e, psg=psg, sa=sa: e.activation(out=sa, in_=psg[:], func=AF.Sigmoid), reads=[rpsg], writes=[ra_])
                P.add("act", lambda e, psg2=psg2, sb_=sb_: e.activation(out=sb_, in_=psg2[:], func=AF.Sigmoid), reads=[rpsg2], writes=[rb_])
                P.add("dve", lambda e, psa=psa, sa=sa: e.tensor_tensor(out=sa, in0=psa[:], in1=sa, op=ALU.mult), reads=[rpsa], writes=[ra_])
                P.add("dve", lambda e, psb=psb, sb_=sb_: e.tensor_tensor(out=sb_, in0=psb[:], in1=sb_, op=ALU.mult), reads=[rpsb], writes=[rb_])
                if pend_add is not None:
                    pend_add()
                pend_add = (lambda c=c, sa=sa, sb_=sb_, ra_=ra_, rb_=rb_: P.add(
                    "dve", lambda e: e.tensor_tensor(out=gT[:, c * 512:(c + 1) * 512], in0=sa, in1=sb_, op=ALU.add),
                    reads=[ra_, rb_], writes=[r_mg]))
            pend_add()
            pend_add = None
            wo.append(wload(wsrc(wout_d, 8, 512, 512, rowlen=D), 8, 512, ringC))
            for tt in range(4):
                i = tcn * 4 + tt
                pss = []
                for hf in range(2):
                    ps, rps = nb()
                    pss.append((ps, rps))
                    for k in range(8):
                        P.add("pe", lambda e, ps=ps, k=k, tt=tt, hf=hf, wo=wo: e.matmul(
                            ps[:], lhsT=gT[:, k * 512 + tt * 128:k * 512 + tt * 128 + 128], rhs=wsl(wo[hf][0], k, 512, 0, 512),
                            start=(k == 0), stop=(k == 7)), reads=[r_mg, wo[hf][1]], writes=[rps])
                residual_epilogue(P, pss, x1tile(i), r_x, gt1bc, r_bc, stat, junk, mhalf, r_const, Res())
        P.barrier()

        dump('x1', RA[:])
        x1src = lambda tcn: [x1tile(tcn * 4 + tt) for tt in range(4)]
        dstn = [RB[:, tt * D:(tt + 1) * D] for tt in range(4)]
        ln_stats(x1src(0), r_x1all, 0)
        for tcn in range(4):
            if tcn + 1 < 4:
                ln_stats(x1src(tcn + 1), r_x1all, tcn + 1)
            ln_apply(x1src(tcn), r_x1all, tcn, vec[:, 24:32], vec[:, 32:40], dstn, r_xn2)
        P.barrier()

        dump('h2T', hT[:])
        r_hh = Res()
        P.add("dve", lambda e: e.tensor_copy(out=bass.AP(h2halo, 0, [[64, 128], [8, 8], [2, 3], [1, 2]]),
                                             in_=bass.AP(hT, 511, [[HT_L, 128], [S, 8], [512, 3], [1, 2]])), reads=hT_all, writes=[r_hh])
        r_E = [[Res(), Res()] for _ in range(22)]
        r_gTj = [Res() for _ in range(22)]
        r_mo = [Res() for _ in range(4)]
        for tcn in range(4):
            pend = None
            for j in range(22):
                g_, jj = j // 4, j % 4
                ncol = 512 if g_ < 5 else 256
                if jj == 0:
                    tv_ = wload(wsrc(wup_d, 8, g_ * 512, ncol, rowlen=5632), 8, ncol, ringF)
                    tg_ = wload(wsrc(wup_d, 8, 2816 + g_ * 512, ncol, rowlen=5632), 8, ncol, ringF)
                st_ = ffs[0] % 3
                ffs[0] += 1
                bufs = (attAf[:, st_ * 1024: st_ * 1024 + 512], attAf[:, st_ * 1024 + 512: st_ * 1024 + 1024])
                ph, rph = pb[6 + j % 2], rb[6 + j % 2]
                mains = []
                for isg in range(2):
                    wt, wr_ = (tv_, tg_)[isg]
                    ps, rps = pb[ffb[0] % 6], rb[ffb[0] % 6]
                    ffb[0] += 1
                    mains.append((ps, rps))
                    for k in range(8):
                        P.add("pe", lambda e, ps=ps, k=k, wt=wt, tcn=tcn, ncol=ncol, jj=jj: e.matmul(
                            ps[:], lhsT=wsl(wt, k, ncol, jj * 128, 128), rhs=hT[:, k * S + tcn * 512:k * S + tcn * 512 + 512],
                            start=(k == 0), stop=(k == 7)), reads=hT_all + [wr_], writes=[rps])
                    if tcn == 0:
                        for k in range(8):
                            P.add("pe", lambda e, ph=ph, k=k, wt=wt, ncol=ncol, jj=jj, isg=isg: e.matmul(
                                ph[:, 8 * isg:8 * isg + 6], lhsT=wsl(wt, k, ncol, jj * 128, 128), rhs=h2halo[:, k * 8:k * 8 + 6],
                                start=(k == 0), stop=(k == 7), skip_group_check=True), reads=[r_hh, wr_], writes=[rph])
                        P.add("act", lambda e, ph=ph, isg=isg, j=j: e.activation(
                            out=Etab[:, (isg * 22 + j) * 6:(isg * 22 + j) * 6 + 6], in_=ph[:, 8 * isg:8 * isg + 6], func=AF.Identity),
                            reads=[rph], writes=[r_E[j][isg]])
                cw = []
                for isg in range(2):
                    ch = isg * 22 + j
                    cw.append((spt[:, SP_CW0 + ch:SP_CW0 + ch + 1], spt[:, SP_CW1 + ch:SP_CW1 + ch + 1],
                               spt[:, SP_CW2 + ch:SP_CW2 + ch + 1], spt[:, SP_CB + ch:SP_CB + ch + 1]))
                for isg in range(2):
                    P.add("act", lambda e, t=bufs[isg], ps=mains[isg][0], w1=cw[isg][1], cb=cw[isg][3]: e.activation(
                        out=t, in_=ps[:], func=AF.Identity, scale=w1, bias=cb), reads=[mains[isg][1], r_const], writes=[r_ff[st_][isg]])
                for isg in range(2):
                    e0 = (isg * 22 + j) * 6
                    if tcn > 0:
                        P.add("act", lambda e, t=bufs[isg], e0=e0, w0=cw[isg][0], tcn=tcn: e.activation(
                            out=t[:, 0:1], in_=Etab[:, e0 + 2 * (tcn - 1):e0 + 2 * (tcn - 1) + 1], func=AF.Identity, scale=w0, bias=t[:, 0:1]),
                            reads=[r_E[j][isg], r_const], writes=[r_ff[st_][isg]])
                    if tcn < 3:
                        P.add("act", lambda e, t=bufs[isg], e0=e0, w2=cw[isg][2], tcn=tcn: e.activation(
                            out=t[:, 511:512], in_=Etab[:, e0 + 2 * tcn + 1:e0 + 2 * tcn + 2], func=AF.Identity, scale=w2, bias=t[:, 511:512]),
                            reads=[r_E[j][isg], r_const], writes=[r_ff[st_][isg]])
                if pend is not None:
                    pend[0]()
                for isg in range(2):
                    P.add("dve", lambda e, t=bufs[isg], ps=mains[isg][0], w0=cw[isg][0]: e.scalar_tensor_tensor(
                        out=t[:, 1:512], in0=ps[:, 0:511], scalar=w0, in1=t[:, 1:512], op0=ALU.mult, op1=ALU.add),
                        reads=[mains[isg][1], r_const], writes=[r_ff[st_][isg]])
                for isg in range(2):
                    P.add("dve", lambda e, t=bufs[isg], ps=mains[isg][0], w2=cw[isg][2]: e.scalar_tensor_tensor(
                        out=t[:, 0:511], in0=ps[:, 1:512], scalar=w2, in1=t[:, 0:511], op0=ALU.mult, op1=ALU.add),
                        reads=[mains[isg][1], r_const], writes=[r_ff[st_][isg]])
                if pend is not None:
                    pend[1]()
                gelu_fn = (lambda bufs=bufs, st_=st_: P.add("act", lambda e: e.activation(out=bufs[1], in_=bufs[1], func=AF.Gelu_apprx_tanh),
                                                           reads=[], writes=[r_ff[st_][1]]))
                mult_fn = (lambda bufs=bufs, st_=st_, j=j: P.add("dve", lambda e: e.tensor_tensor(
                    out=gT[:, j * 512:(j + 1) * 512], in0=bufs[1], in1=bufs[0], op=ALU.mult), reads=[r_ff[st_][0], r_ff[st_][1]], writes=[r_gTj[j]]))
                pend = (gelu_fn, mult_fn)
            pend[0]()
            pend[1]()
            if pend_epi[0] is not None:
                pend_epi[0](None)
                pend_epi[0] = None
            mo_res = r_mo
            for hf in range(2):
                pss = [nb() for _ in range(4)]
                for (k0, kc) in ((0, 8), (8, 8), (16, 6)):
                    wt, wr_ = wload(wsrc(wdn_d, 22, hf * 512, 512, k0=k0, kc=kc, rowlen=D), kc, 512, ringF)
                    for tt in range(4):
                        ps, rps = pss[tt]
                        for kk in range(kc):
                            k = k0 + kk
                            P.add("pe", lambda e, ps=ps, k=k, kk=kk, tt=tt, wt=wt: e.matmul(
                                ps[:], lhsT=gT[:, k * 512 + tt * 128:k * 512 + tt * 128 + 128], rhs=wsl(wt, kk, 512, 0, 512),
                                start=(k == 0), stop=(k == 21), skip_group_check=True), reads=[r_gTj[k], wr_], writes=[rps])
                for tt in range(4):
                    ps, rps = pss[tt]
                    P.add("act", lambda e, ps=ps, tt=tt, hf=hf: e.activation(out=RB[:, tt * D + hf * 512: tt * D + hf * 512 + 512], in_=ps[:], func=AF.Identity),
                          reads=[rps], writes=[mo_res[tt]])
            def epi(tt_sel, tcn=tcn, s=s, mo_res=mo_res):
                for tt in ([tt_sel] if tt_sel is not None else range(4)):
                    i = tcn * 4 + tt
                    r_y = Res()
                    residual_epilogue(P, [(RB[:, tt * D: tt * D + 512], mo_res[tt]), (RB[:, tt * D + 512: tt * D + 1024], mo_res[tt])],
                                      x1tile(i), Res(), gt2bc, r_bc, stat, junk, mhalf, r_const, r_y, sb=True)
                    dsty = bass.AP(y_d, (s * S + i * 128) * D, [[D, 128], [1, D]])
                    P.add("pool", lambda e, dsty=dsty, i=i: e.dma_start(out=dsty, in_=x1tile(i)), reads=[r_y], dma=True)
            pend_epi[0] = epi
        pend_epi[0](None)
        pend_epi[0] = None
        P.barrier()
    P.barrier()
    P.emit()
    return nc


_ctr = [0]
_rst = [None]


def residual_epilogue(P, halves, xt, r_x, gbc, r_bc, stat, junk, mhalf, r_const, r_out, sb=False):
    if _rst[0] is None:
        _rst[0] = [Res() for _ in range(5)]
    r_junk = _rst[0][4]
    ci = _ctr[0] % 4
    c = ci * 4
    _ctr[0] += 1
    r_st = _rst[0][ci]
    srcs = [(ps if sb else ps[:], rps) for (ps, rps) in halves]
    for hf, (src, rps) in enumerate(srcs):
        P.add("act", lambda e, src=src, hf=hf: e.activation(out=junk[:, 0:512], in_=src, func=AF.Square,
                                                            accum_out=stat[:, 32 + c + hf:33 + c + hf]), reads=[rps], writes=[r_st, r_junk])
    P.add("dve", lambda e: e.tensor_tensor(out=stat[:, 34 + c:35 + c], in0=stat[:, 32 + c:33 + c], in1=stat[:, 33 + c:34 + c], op=ALU.add),
          reads=[r_st], writes=[r_st])
    P.add("dve", lambda e: e.tensor_scalar(out=stat[:, 35 + c:36 + c], in0=stat[:, 34 + c:35 + c], scalar1=1.0 / D, scalar2=EPS,
                                           op0=ALU.mult, op1=ALU.add), reads=[r_st], writes=[r_st])
    P.add("pool", lambda e: e.tensor_tensor(out=stat[:, 48 + ci:49 + ci], in0=stat[:, 35 + c:36 + c], in1=mhalf[:, 0:1], op=ALU.pow),
          reads=[r_st, r_const], writes=[r_st])
    for hf, (src, rps) in enumerate(srcs):
        P.add("dve", lambda e, src=src, hf=hf: e.scalar_tensor_tensor(out=src, in0=src, scalar=stat[:, 48 + ci:49 + ci],
                                                                   in1=gbc[:, hf * 512:(hf + 1) * 512], op0=ALU.mult, op1=ALU.mult),
              reads=[r_st, r_bc], writes=[rps])
        P.add("dve", lambda e, src=src, hf=hf: e.tensor_tensor(out=xt[:, hf * 512:(hf + 1) * 512], in0=src, in1=xt[:, hf * 512:(hf + 1) * 512], op=ALU.add),
              reads=[rps, r_x], writes=[r_out])


_NC = [None]


def _prep(x_prompt, x_sample, c_prompt, c_sample, w_ada, b_ada, g_mix_pre, g_mix_post, g_ffn_pre, g_ffn_post,
          w_in, rpb, w_branch_a, w_branch_b, w_out, w_up, conv_w, conv_b, w_down):
    f = lambda a: np.ascontiguousarray(np.asarray(a, dtype=np.float32))
    xs = np.concatenate([f(x_prompt), f(x_sample)], axis=0)
    cs = np.concatenate([f(c_prompt), f(c_sample)], axis=0)
    COS, SIN, perm = _rope_tables()
    rr, cr, m1, m2 = _na_index()
    consts = np.zeros((128, NCC), np.float32)
    consts[:, C_ID:C_ID + 128] = np.eye(128, dtype=np.float32)
    consts[:, C_PERM:C_PERM + 128] = perm
    xx = np.arange(384)[None, :]
    pp = np.arange(128)[:, None]
    consts[:, C_T:C_T + 384] = np.where(np.abs(xx - pp - 64) <= 64, 0.0, NEG)
    consts[:, C_M1:C_M1 + 896] = np.where(m1, 1e30, NEG)
    consts[:, C_A1:C_A1 + 896] = np.where(m2, 1e30, NEG)
    consts[:, C_COS:C_COS + S] = COS
    consts[:, C_SIN:C_SIN + S] = SIN
    consts[:, C_ONES:C_ONES + 128] = 1.0
    G = np.ascontiguousarray(f(rpb)[0][:, rr, cr])
    spb = np.zeros((128, NSP), np.float32)
    spb[:, SP_GPRE:SP_GPRE + 8] = _fm(f(g_mix_pre)[0], 8)
    spb[:, SP_GPOST:SP_GPOST + 8] = _fm(f(g_mix_post)[0], 8)
    spb[:, SP_FPRE:SP_FPRE + 8] = _fm(f(g_ffn_pre)[0], 8)
    spb[:, SP_FPOST:SP_FPOST + 8] = _fm(f(g_ffn_post)[0], 8)
    spb[:, SP_BADA:SP_BADA + 48] = _fm(f(b_ada)[0], 48)
    cw = f(conv_w)[0]
    spb[:, SP_CW0:SP_CW0 + 44] = _fm(cw[0], 44)
    spb[:, SP_CW1:SP_CW1 + 44] = _fm(cw[1], 44)
    spb[:, SP_CW2:SP_CW2 + 44] = _fm(cw[2], 44)
    spb[:, SP_CB:SP_CB + 44] = _fm(f(conv_b)[0], 44)
    shared = {"consts": consts, "G": G, "w_ada": f(w_ada)[0], "w_in": f(w_in)[0], "w_ba": f(w_branch_a)[0],
              "w_bb": f(w_branch_b)[0], "w_out": f(w_out)[0], "w_up": f(w_up)[0], "w_down": f(w_down)[0]}
    in_maps = []
    for i in range(8):
        sp_i = spb.copy()
        ci = cs[3 * i:3 * i + 3]
        sp_i[:, SP_C:SP_C + 24] = ci.reshape(3, 8, 128).transpose(2, 1, 0).reshape(128, 24)
        m = dict(shared)
        m["x"] = np.ascontiguousarray(xs[3 * i:3 * i + 3])
        m["sp"] = sp_i
        in_maps.append(m)
    return in_maps


def kernel(**inputs):
    in_maps = _prep(**inputs)
    if _NC[0] is None:
        _NC[0] = build()
    nc = _NC[0]
    res = run_bass_kernel_spmd(nc, in_maps, core_ids=list(range(8)))
    ys = np.concatenate([np.asarray(r["y"], dtype=np.float32) for r in res.results], axis=0)
    return ys[:8].copy(), ys[8:].copy()
```
